# Optimizing a Trainium2 kernel written in Bass

```python
import math
import jax, jax.numpy as jnp
from jax import lax
import numpy as np

D_MODEL = 1024
BATCH = 32
SEQ = 256
DEPTH = 4
DEC_BATCH = 4
DEC_SEQ = 1024
PAST_LEN = 256

GRID_W = 64
LRU_WIDTH = 1024
LRU_HEADS = 8
LRU_BLOCK = LRU_WIDTH // LRU_HEADS
LRU_CONV = 4
LRU_C = 8.0
CONF_WIDTH = 1024
CONF_KERNEL = 31
CHUNK = 128
SGU_WIDTH = 2048
SGU_HEADS = 8
SGU_HEAD_DIM = SGU_WIDTH // SGU_HEADS
D_FF = 2816
FFN_CONV = 3
N_AB = (DEPTH + 1) // 2
N_C = DEPTH // 2
N_MOD = 6
EPS = 1e-6
POS_BASE = 10000.0

kernel_name = "hybrid_rglru_conformer_sgu_diffusion_step"


def rms_norm(x, g):
    xf = x.astype(jnp.float32)
    y = xf * lax.rsqrt(jnp.mean(xf * xf, axis=-1, keepdims=True) + EPS)
    return (y * g.astype(jnp.float32)).astype(x.dtype)


def layer_norm(x, g, b):
    xf = x.astype(jnp.float32)
    mu = jnp.mean(xf, axis=-1, keepdims=True)
    var = jnp.mean(jnp.square(xf - mu), axis=-1, keepdims=True)
    y = (xf - mu) * lax.rsqrt(var + EPS)
    return (y * g.astype(jnp.float32) + b.astype(jnp.float32)).astype(x.dtype)


def dw_conv(x, w, b, pad_left, pad_right):
    out = lax.conv_general_dilated(
        x, w[:, None, :].astype(x.dtype), window_strides=(1,),
        padding=[(pad_left, pad_right)], dimension_numbers=('NWC', 'WIO', 'NWC'),
        feature_group_count=x.shape[-1])
    return out + b.astype(x.dtype)


def grid_pos_embedding(t, d, dtype):
    rows = t // GRID_W
    row = jnp.repeat(jnp.arange(rows), GRID_W).astype(jnp.float32)
    col = jnp.tile(jnp.arange(GRID_W), rows).astype(jnp.float32)
    quarter = d // 4
    freq = jnp.exp(-math.log(POS_BASE) * jnp.arange(quarter, dtype=jnp.float32) / quarter)

    def enc(p):
        ang = p[:, None] * freq[None, :]
        return jnp.concatenate([jnp.sin(ang), jnp.cos(ang)], axis=-1)

    return jnp.concatenate([enc(row), enc(col)], axis=-1).astype(dtype)


def linear_scan(a, b, reverse):
    def combine(p, q):
        a1, b1 = p
        a2, b2 = q
        return a1 * a2, a2 * b1 + b2
    return lax.associative_scan(combine, (a, b), reverse=reverse, axis=1)


def mixer_ab(xin, w_in, lru_conv_w, lru_conv_b, lru_w_gates, lru_b_gates, lru_lambda,
             conf_conv_w, conf_conv_b, conf_ln_g, conf_ln_b, w_out, h0):
    z = xin @ w_in
    gate_br, x_br, glu_a, glu_b = jnp.split(
        z, [LRU_WIDTH, 2 * LRU_WIDTH, 2 * LRU_WIDTH + CONF_WIDTH], axis=-1)
    xc = dw_conv(x_br, lru_conv_w, lru_conv_b, LRU_CONV // 2, LRU_CONV - 1 - LRU_CONV // 2)
    bsz, t, _ = xc.shape
    xh = xc.reshape(bsz, t, LRU_HEADS, LRU_BLOCK)
    gl = jnp.einsum('bthi,dghij->dgbthj', xh, lru_w_gates).reshape(2, 2, bsz, t, LRU_WIDTH)
    gates = jax.nn.sigmoid((gl + lru_b_gates[:, :, None, None, :]).astype(jnp.float32))
    r, i = gates[:, 0], gates[:, 1]
    log_a = -LRU_C * r * jax.nn.softplus(-lru_lambda.astype(jnp.float32))[:, None, None, :]
    a = jnp.exp(log_a)
    bt = jnp.sqrt(-jnp.expm1(2.0 * log_a)) * i * xc.astype(jnp.float32)[None]
    dec_f, hf = linear_scan(a[0], bt[0], False)
    dec_b, hb = linear_scan(a[1], bt[1], True)
    if h0 is not None:
        hf = hf + dec_f * h0[:, 0, None, :].astype(jnp.float32)
        hb = hb + dec_b * h0[:, 1, None, :].astype(jnp.float32)
    y_a = jax.nn.gelu(gate_br) * (hf + hb).astype(xin.dtype)
    g = glu_a * jax.nn.sigmoid(glu_b)
    g = dw_conv(g, conf_conv_w, conf_conv_b, CONF_KERNEL // 2, CONF_KERNEL // 2)
    y_b = jax.nn.silu(layer_norm(g, conf_ln_g, conf_ln_b))
    out = jnp.concatenate([y_a, y_b], axis=-1) @ w_out
    return out, hf[:, -1], hb[:, 0]


def mixer_c(xin, w_in, ln_g, ln_b, w_s, b_s, w_out):
    z = jax.nn.gelu(xin @ w_in)
    u, v = jnp.split(z, 2, axis=-1)
    v = layer_norm(v, ln_g, ln_b)
    bsz, t, _ = v.shape
    vc = v.reshape(bsz, t // CHUNK, CHUNK, SGU_HEADS, SGU_HEAD_DIM)
    s = jnp.einsum('bnphd,hqp->bnqhd', vc, w_s) + b_s.T[None, None, :, :, None]
    return (u * s.reshape(bsz, t, SGU_WIDTH)) @ w_out


def conv_ffn(xin, w_up, conv_w, conv_b, w_down):
    z = dw_conv(xin @ w_up, conv_w, conv_b, FFN_CONV // 2, FFN_CONV // 2)
    gate, val = jnp.split(z, 2, axis=-1)
    return (jax.nn.gelu(gate) * val) @ w_down


def trunk(x, cond, h0_all, mod_w, mod_b, norm_g,
          ab_w_in, lru_conv_w, lru_conv_b, lru_w_gates, lru_b_gates, lru_lambda,
          conf_conv_w, conf_conv_b, conf_ln_g, conf_ln_b, ab_w_out,
          c_w_in, c_ln_g, c_ln_b, c_w_s, c_b_s, c_w_out,
          ffn_w_up, ffn_conv_w, ffn_conv_b, ffn_w_down):
    sc = jax.nn.silu(cond)
    states = []
    for l in range(DEPTH):
        m = (sc @ mod_w[l] + mod_b[l])[:, None, :]
        sh1, sc1, g1, sh2, sc2, g2 = jnp.split(m, N_MOD, axis=-1)
        h = rms_norm(x, norm_g[l, 0]) * (1.0 + sc1) + sh1
        j = l // 2
        if l % 2 == 0:
            h0 = None if h0_all is None else h0_all[:, j]
            h, hf_last, hb_first = mixer_ab(
                h, ab_w_in[j], lru_conv_w[j], lru_conv_b[j], lru_w_gates[j], lru_b_gates[j],
                lru_lambda[j], conf_conv_w[j], conf_conv_b[j], conf_ln_g[j], conf_ln_b[j],
                ab_w_out[j], h0)
            states.append(jnp.stack([hf_last, hb_first], axis=1))
        else:
            h = mixer_c(h, c_w_in[j], c_ln_g[j], c_ln_b[j], c_w_s[j], c_b_s[j], c_w_out[j])
        x = x + g1 * rms_norm(h, norm_g[l, 1])
        h = rms_norm(x, norm_g[l, 2]) * (1.0 + sc2) + sh2
        h = conv_ffn(h, ffn_w_up[l], ffn_conv_w[l], ffn_conv_b[l], ffn_w_down[l])
        x = x + g2 * rms_norm(h, norm_g[l, 3])
    return x, jnp.stack(states, axis=1)


def setup_inputs(seed: int = 0) -> dict:
    key = jax.random.key(seed)
    ks = iter(jax.random.split(key, 40))

    def nrm(shape, scale):
        return jax.random.normal(next(ks), shape, jnp.float32) * scale

    D = D_MODEL
    a0 = jax.random.uniform(next(ks), (N_AB, 2, LRU_WIDTH), jnp.float32, 0.9, 0.999)
    sig = a0 ** (1.0 / LRU_C)
    lru_lambda = jnp.log(sig) - jnp.log1p(-sig)
    return {
        "x_prompt": nrm((BATCH, SEQ, D), 1.0),
        "x_sample": nrm((DEC_BATCH, DEC_SEQ, D), 1.0),
        "state_lru": nrm((DEC_BATCH, N_AB, 2, LRU_WIDTH), 0.5),
        "c": nrm((DEC_BATCH, D), 1.0),
        "c_ctx": nrm((D,), 1.0),
        "mod_w": nrm((DEPTH, D, N_MOD * D), 0.5 * D ** -0.5),
        "mod_b": nrm((DEPTH, N_MOD * D), 0.02),
        "norm_g": 1.0 + nrm((DEPTH, 4, D), 0.02),
        "ab_w_in": nrm((N_AB, D, 2 * LRU_WIDTH + 2 * CONF_WIDTH), D ** -0.5),
        "lru_conv_w": nrm((N_AB, LRU_CONV, LRU_WIDTH), LRU_CONV ** -0.5),
        "lru_conv_b": nrm((N_AB, LRU_WIDTH), 0.02),
        "lru_w_gates": nrm((N_AB, 2, 2, LRU_HEADS, LRU_BLOCK, LRU_BLOCK), LRU_BLOCK ** -0.5),
        "lru_b_gates": nrm((N_AB, 2, 2, LRU_WIDTH), 0.02),
        "lru_lambda": lru_lambda,
        "conf_conv_w": nrm((N_AB, CONF_KERNEL, CONF_WIDTH), CONF_KERNEL ** -0.5),
        "conf_conv_b": nrm((N_AB, CONF_WIDTH), 0.02),
        "conf_ln_g": 1.0 + nrm((N_AB, CONF_WIDTH), 0.02),
        "conf_ln_b": nrm((N_AB, CONF_WIDTH), 0.02),
        "ab_w_out": nrm((N_AB, LRU_WIDTH + CONF_WIDTH, D), (LRU_WIDTH + CONF_WIDTH) ** -0.5),
        "c_w_in": nrm((N_C, D, 2 * SGU_WIDTH), D ** -0.5),
        "c_ln_g": 1.0 + nrm((N_C, SGU_WIDTH), 0.02),
        "c_ln_b": nrm((N_C, SGU_WIDTH), 0.02),
        "c_w_s": nrm((N_C, SGU_HEADS, CHUNK, CHUNK), CHUNK ** -0.5),
        "c_b_s": 1.0 + nrm((N_C, SGU_HEADS, CHUNK), 0.02),
        "c_w_out": nrm((N_C, SGU_WIDTH, D), SGU_WIDTH ** -0.5),
        "ffn_w_up": nrm((DEPTH, D, 2 * D_FF), D ** -0.5),
        "ffn_conv_w": nrm((DEPTH, FFN_CONV, 2 * D_FF), FFN_CONV ** -0.5),
        "ffn_conv_b": nrm((DEPTH, 2 * D_FF), 0.02),
        "ffn_w_down": nrm((DEPTH, D_FF, D), D_FF ** -0.5),
    }


def reference(x_prompt, x_sample, state_lru, c, c_ctx, mod_w, mod_b, norm_g,
              ab_w_in, lru_conv_w, lru_conv_b, lru_w_gates, lru_b_gates, lru_lambda,
              conf_conv_w, conf_conv_b, conf_ln_g, conf_ln_b, ab_w_out,
              c_w_in, c_ln_g, c_ln_b, c_w_s, c_b_s, c_w_out,
              ffn_w_up, ffn_conv_w, ffn_conv_b, ffn_w_down):
    y_prompt, new_state_lru = trunk(
        x_prompt, c_ctx[None, :], None, mod_w, mod_b, norm_g,
        ab_w_in, lru_conv_w, lru_conv_b, lru_w_gates, lru_b_gates, lru_lambda,
        conf_conv_w, conf_conv_b, conf_ln_g, conf_ln_b, ab_w_out,
        c_w_in, c_ln_g, c_ln_b, c_w_s, c_b_s, c_w_out,
        ffn_w_up, ffn_conv_w, ffn_conv_b, ffn_w_down)
    t = x_sample.shape[1]
    xs = x_sample + grid_pos_embedding(t, x_sample.shape[-1], x_sample.dtype)[None]
    y_sample, _ = trunk(
        xs, c, state_lru, mod_w, mod_b, norm_g,
        ab_w_in, lru_conv_w, lru_conv_b, lru_w_gates, lru_b_gates, lru_lambda,
        conf_conv_w, conf_conv_b, conf_ln_g, conf_ln_b, ab_w_out,
        c_w_in, c_ln_g, c_ln_b, c_w_s, c_b_s, c_w_out,
        ffn_w_up, ffn_conv_w, ffn_conv_b, ffn_w_down)
    return (y_prompt, y_sample, new_state_lru)
```

```python
import math
import numpy as np
import concourse.bass as bass
import concourse.mybir as mybir
from concourse.bass_utils import run_bass_kernel_spmd

F32 = mybir.dt.float32
I32 = mybir.dt.int32
BF16 = mybir.dt.bfloat16
AF = mybir.ActivationFunctionType
ALU = mybir.AluOpType
AX = mybir.AxisListType

NT = 1536
NSEG = 6
EPS = 1e-6
SB_BASE = 16512
SB_END = 229376
RING_SLOTS = 3
SCHEDULE = False
MODEL_REPORT = False
SCHED_WINDOW = 2000
PART2_DELAY = 4
NSIDE = 3
SLOT_BYTES = 5632

PV_SPEC = [
    ("norm_g", (4, 4, 8)), ("mod_b", (4, 48)), ("lru_cw", (2, 4, 8)), ("lru_cb", (2, 8)),
    ("lru_bg", (2, 2, 2, 8)), ("lru_lam", (2, 2, 8)), ("cf_cw", (2, 31, 8)), ("cf_cb", (2, 8)),
    ("cf_g", (2, 8)), ("cf_b", (2, 8)), ("c_lng", (2, 16)), ("f_cw", (4, 3, 44)), ("f_cb", (4, 44)),
    ("h0", (2, 2, 8)), ("cond", (8, 2)), ("pm", (1,)),
]
PV_OFF = {}
_o = 0
for _n, _s in PV_SPEC:
    PV_OFF[_n] = (_o, _s)
    _o += int(np.prod(_s))
NPV = _o


def pv_index(name, *idx):
    off, shp = PV_OFF[name]
    flat = 0
    for i, n in zip(idx, shp):
        flat = flat * n + i
    rem = int(np.prod(shp[len(idx):])) if len(idx) < len(shp) else 1
    return off + flat * rem, rem


class View:
    __slots__ = ("ap", "sp", "lo", "hi", "n")

    def __init__(self, ap, sp, lo, hi, n=512):
        self.ap, self.sp, self.lo, self.hi, self.n = ap, sp, lo, hi, n


def rev(v):
    return View(v.ap[:, ::-1], v.sp, v.lo, v.hi, v.n)


class Buf:
    def __init__(self, handle, shape, esz, base, sp):
        self.t = handle
        self.shape = list(shape)
        self.esz = esz
        self.base = base
        self.sp = sp
        st = []
        s = 1
        for n in reversed(self.shape):
            st.append(s)
            s *= n
        self.strides = list(reversed(st))
        self.nbytes = s * esz

    def v(self, *idx, parts=None):
        key = [slice(None) if parts is None else parts]
        lo = 0
        hi = 0
        cnt = 1
        for d, (n, st) in enumerate(zip(self.shape, self.strides)):
            i = idx[d] if d < len(idx) else None
            if i is None:
                i = slice(None)
            if isinstance(i, int):
                a = b = i
            else:
                a, b, step = i.indices(n)
                assert step > 0 and b > a, (i, n)
                cnt *= (b - a - 1) // step + 1
                b = a + ((b - a - 1) // step) * step
            key.append(i)
            lo += a * st
            hi += b * st
        blo = self.base + lo * self.esz
        bhi = self.base + (hi + 1) * self.esz
        if self.sp == "ps":
            blo = blo // 2048 * 2048
            bhi = (bhi + 2047) // 2048 * 2048
        return View(self.t.ap()[tuple(key)], self.sp, blo, bhi, cnt)


class Ins:
    __slots__ = ("eng", "emit", "deps", "order", "flag", "ms", "dma_sem", "dma_val", "key", "gi", "dur", "tbl", "lat", "tag")

    def __init__(self, eng, emit):
        self.eng = eng
        self.emit = emit
        self.deps = set()
        self.order = set()
        self.flag = False
        self.ms = 0
        self.dma_sem = None
        self.dma_val = 0
        self.key = eng
        self.gi = 0
        self.dur = 0.3
        self.tbl = None
        self.lat = 0.0


class Prog:
    ENGS = ("pe", "act", "dve", "pool", "sp")

    def __init__(self):
        self.q = {e: [] for e in self.ENGS}
        self.recs = {"sb": {}, "ps": {}}
        self.dma_count = {}
        self.all = []
        self.tag = ''

    def _track(self, ins, reads, writes):
        seen = set()
        for v, w in [(r, False) for r in reads] + [(x, True) for x in writes]:
            k = (v.sp, v.lo, v.hi, w)
            if k in seen:
                continue
            seen.add(k)
            for key, tok in self.recs[v.sp].items():
                if key[0] < v.hi and v.lo < key[1] and (key[2] or w):
                    if tok is ins:
                        continue
                    if tok.eng == "pe" and ins.eng == "pe" and tok.dma_sem is None:
                        ins.order.add(tok)
                        continue
                    ins.deps.add(tok)
                    tok.flag = True
        for v in writes:
            recs = self.recs[v.sp]
            dead = [k for k in recs if k[0] >= v.lo and k[1] <= v.hi]
            for k in dead:
                del recs[k]
            recs[(v.lo, v.hi, True, ins.key)] = ins
        for v in reads:
            k = (v.lo, v.hi, False, ins.key)
            old = self.recs[v.sp].get(k)
            if old is not None and old is not ins:
                ins.order.add(old)
            self.recs[v.sp][k] = ins

    def _add(self, ins):
        ins.gi = len(self.all)
        ins.tag = self.tag
        self.all.append(ins)
        self.q[ins.eng].append(ins)

    def op(self, eng, reads, writes, emit, dur=0.3, tbl=None):
        ins = Ins(eng, emit)
        ins.dur = dur
        ins.tbl = tbl
        self._track(ins, reads, writes)
        self._add(ins)
        return ins

    def dma(self, queue, sem, pairs, reads, writes):
        n = self.dma_count.get(sem, 0) + len(pairs)
        self.dma_count[sem] = n

        def emit(e, pairs=pairs, sem=sem):
            bi = None
            for o, i in pairs:
                bi = e.dma_start(out=o, in_=i)
                bi.then_inc(sem, 16)
            return None

        ins = Ins(queue, emit)
        ins.dma_sem = sem
        ins.dma_val = 16 * n
        ins.key = ("dma", id(sem))
        nel = sum(v.n for v in (writes or reads))
        ins.dur = 0.6
        ins.lat = 2.0 + nel * 512 / 300e3
        self._track(ins, reads, writes)
        self._add(ins)
        return ins

    def schedule(self):
        import heapq
        N = len(self.all)
        preds = [set() for _ in range(N)]
        for ins in self.all:
            for d in ins.deps:
                preds[ins.gi].add(d.gi)
            for d in ins.order:
                preds[ins.gi].add(d.gi)
        for e in ("pool", "sp"):
            qq = self.q[e]
            for a_, b_ in zip(qq, qq[1:]):
                preds[b_.gi].add(a_.gi)
        succ = [[] for _ in range(N)]
        indeg = [0] * N
        for i in range(N):
            indeg[i] = len(preds[i])
            for p in preds[i]:
                succ[p].append(i)
        blevel = [0.0] * N
        for i in range(N - 1, -1, -1):
            ins = self.all[i]
            m = 0.0
            for sidx in succ[i]:
                if blevel[sidx] > m:
                    m = blevel[sidx]
            blevel[i] = ins.dur + ins.lat + 0.12 + m
        ready_t = [0.0] * N
        ready = {e: [] for e in self.ENGS}
        for ins in self.all:
            if indeg[ins.gi] == 0:
                ready[ins.eng].append(ins.gi)
        eng_free = {e: 0.0 for e in self.ENGS}
        cur_tbl = [None]
        newq = {e: [] for e in self.ENGS}
        done = 0
        WINDOW = SCHED_WINDOW
        low = 0
        sched = [False] * N
        while done < N:
            best = None
            for e in self.ENGS:
                r = ready[e]
                if not r:
                    continue
                tmin = min(ready_t[g] for g in r)
                t_e = max(eng_free[e], tmin)
                pick = None
                for g in r:
                    if ready_t[g] <= t_e + 1e-9 and g <= low + WINDOW:
                        key = (-blevel[g], g)
                        if pick is None or key < pick[0]:
                            pick = (key, g)
                if pick is None:
                    g = min(r)
                    t_e = max(eng_free[e], ready_t[g])
                    pick = (None, g)
                cand = (t_e, pick[1], e)
                if best is None or cand < best:
                    best = cand
            st, gi, e = best
            ready[e].remove(gi)
            ins = self.all[gi]
            if e == "act" and ins.tbl is not None and ins.tbl != cur_tbl[0]:
                st += 1.3
                cur_tbl[0] = ins.tbl
            fin = st + ins.dur
            eng_free[e] = fin
            newq[e].append(ins)
            sched[gi] = True
            while low < N and sched[low]:
                low += 1
            done += 1
            avail = fin + ins.lat + 0.12
            for sidx in succ[gi]:
                if ready_t[sidx] < avail:
                    ready_t[sidx] = avail
                indeg[sidx] -= 1
                if indeg[sidx] == 0:
                    ready[self.all[sidx].eng].append(sidx)
        self.q = newq
        return max(eng_free.values())

    def simulate(self, report=None):
        fin_t = {}
        ptr = {e: 0 for e in self.ENGS}
        eng_free = {e: 0.0 for e in self.ENGS}
        cur_tbl = None
        left = sum(len(q) for q in self.q.values())
        while left:
            progressed = False
            for e in self.ENGS:
                q = self.q[e]
                while ptr[e] < len(q):
                    ins = q[ptr[e]]
                    pr = list(ins.deps) + list(ins.order)
                    if any(p.gi not in fin_t for p in pr):
                        break
                    st = max([eng_free[e]] + [fin_t[p.gi] + p.lat + 0.12 for p in pr])
                    if report is not None and e == "pe" and st - eng_free[e] > report[2] and report[0] <= st <= report[1]:
                        cp = max(pr, key=lambda p: fin_t[p.gi] + p.lat)
                        print(f"  PE idle {eng_free[e]:8.1f}->{st:8.1f} [{ins.tag}] waits on {cp.eng}:{cp.tag} (dur {cp.dur:.2f}, dma={cp.dma_sem is not None})")
                    if e == "act" and ins.tbl is not None and ins.tbl != cur_tbl:
                        st += 1.3
                        cur_tbl = ins.tbl
                    fin_t[ins.gi] = st + ins.dur
                    eng_free[e] = st + ins.dur
                    ptr[e] += 1
                    left -= 1
                    progressed = True
            assert progressed, "deadlock in queue order"
        return max(eng_free.values())


def build(n_layers=4):
    nc = bass.Bass("TRN2", target_bir_lowering=False)

    def dr(name, shape, kind="ExternalInput"):
        return nc.dram_tensor(name, list(shape), F32, kind=kind).ap()

    d_xT = dr("xT", [1024, NT])
    d_pv = dr("pv", [128, NPV])
    d_mask = dr("mask", [128, 80])
    d_ident = dr("ident", [128, 128])
    d_mod_w = dr("mod_w", [4, 24, 128, 2048])
    d_ab_g = dr("ab_g", [2, 4, 128, 2048])
    d_ab_x = dr("ab_x", [2, 8, 128, 1024])
    d_ab_glu = dr("ab_glu", [2, 8, 128, 2048])
    d_ab_out = dr("ab_w_out", [2, 8, 128, 2048])
    d_c_v = dr("c_v", [2, 8, 128, 2048])
    d_c_u = dr("c_u", [2, 8, 128, 2048])
    d_c_out = dr("c_w_out", [2, 8, 128, 2048])
    d_up = dr("ffn_w_up", [4, 22, 128, 2048])
    d_down = dr("ffn_w_down", [4, 8, 128, 2816])
    d_wg = dr("lru_wg", [2, 2, 2, 8, 128, 128])
    d_wst = dr("wst", [2, 128, 8, 128])
    d_cbs = dr("c_bs", [2, 8, 128])
    d_clnb = dr("c_lnb", [2, 2048])
    d_yT = dr("yT", [1024, NT], kind="ExternalOutput")
    d_st = dr("st", [24, 1024], kind="ExternalOutput")

    P = Prog()
    cnt = [0]

    def sbuf(shape, dtype, offset):
        cnt[0] += 1
        esz = 2 if dtype == BF16 else 4
        h = nc.alloc_sbuf_tensor_at(f"b{cnt[0]}", [128] + list(shape), dtype, offset=offset)
        b = Buf(h, shape, esz, offset, "sb")
        assert offset + b.nbytes <= SB_END, (shape, offset)
        return b

    ptr = [SB_BASE]

    def alloc(shape, dtype):
        off = (ptr[0] + 31) // 32 * 32
        b = sbuf(shape, dtype, off)
        ptr[0] = off + b.nbytes
        return b

    def alias(buf, shape, dtype):
        return sbuf(shape, dtype, buf.base)

    X = alloc([8, NT], F32)
    H = alloc([8, NT], BF16)
    RSTD = alloc([NT], F32)
    SQT = alloc([8, 512], BF16)
    SQT3 = alias(SQT, [8, 2, 256], BF16)
    RING = alloc([RING_SLOTS, SLOT_BYTES // 2], BF16)
    PV = alloc([NPV], F32)
    IDF = alloc([128], F32)
    IDENT = alloc([128], BF16)
    ONES = alloc([128], BF16)
    ONE1 = alloc([128], BF16)
    MASK = alloc([5, 16], F32)
    SC = alloc([8, 2], BF16)
    SCF = alias(SC, [16], BF16)
    MODS = [alloc([96], F32), alloc([96], F32)]
    DERS = [alloc([4, 8, 2], F32), alloc([4, 8, 2], F32)]
    TMP = alloc([2, 512], F32)
    TMP3 = alias(TMP, [2, 2, 256], F32)
    SQW = alloc([3, 512], BF16)
    LC = alloc([5, 16], F32)
    BGH = alloc([32], F32)
    ST = alloc([8, 24], F32)
    STT_ = alias(SQT, [1024], F32)
    INIT = alloc([2, 6], F32)
    PHASE = (ptr[0] + 31) // 32 * 32

    ptr[0] = PHASE
    YA = alloc([8, NT], BF16)
    GP = alloc([8, 6, 286], BF16)
    DCONF = alloc([31, 128], BF16)
    TG = alloc([NT], BF16)
    TG3 = alias(TG, [6, 256], BF16)
    WGC = alloc([2, 4, 128], BF16)
    DLC = alloc([1, 4, 128], BF16)
    XBP = alloc([6, 259], BF16)
    HF = alias(RSTD, [NT], F32)
    LT = alloc([3, 3, 512], F32)
    LNT = alias(LT, [3, 512], F32)
    LNT3 = alias(LT, [3, 2, 256], F32)
    XCB2 = alias(SQT, [2, NT], BF16)
    HB = alias(TMP, [2, 512], F32)
    assert ptr[0] <= SB_END, ptr[0]

    ptr[0] = PHASE
    U = alloc([22, NT], BF16)
    ZG = alloc([6, 258], BF16)
    ZV = alloc([6, 258], BF16)
    DF = alloc([2, 3, 128], BF16)
    GT = alloc([3, 512], BF16)
    OT = alloc([3, 512], F32)
    OT3 = alias(OT, [3, 2, 256], F32)
    assert ptr[0] <= SB_END

    ptr[0] = PHASE
    V = alloc([16, 12, 128], BF16)
    US = alias(V, [16, NT], BF16)
    WST = alloc([8, 128], BF16)
    BT = alloc([16, 1, 128], F32)
    LB = alloc([16, 128], F32)
    RB = alloc([8, 128], F32)
    S1T = alloc([3, 512], F32)
    S1T4 = alias(S1T, [3, 4, 128], F32)
    UG = alloc([3, 512], BF16)
    CS1 = alloc([12, 8], F32)
    CS2 = alloc([12, 8], F32)
    CST = alloc([6, 12], F32)
    JUNK = alloc([2, 128], BF16)
    JUNKF = alloc([2, 256], F32)
    JUNKF3 = alias(JUNKF, [2, 2, 128], F32)
    assert ptr[0] <= SB_END

    ptr[0] = PHASE
    ROWV = alloc([16], F32)
    COLV = alloc([64], F32)
    PIDX = alloc([2], F32)
    FREQ = alloc([2], F32)
    PANG = alloc([64], F32)
    PKF = alloc([64], F32)
    PKI = alloc([64], I32)
    PSNR = alloc([16, 1], F32)
    PSNC = alloc([1, 64], F32)
    X3 = alias(X, [8, 24, 64], F32)

    ps_h = nc.alloc_psum_tensor("ps", [128, 8, 512], F32)
    PS = Buf(ps_h, [8, 512], 4, 0, "ps")
    PS3 = Buf(ps_h.reshape([128, 8, 2, 256]), [8, 2, 256], 4, 0, "ps")
    PS4 = Buf(ps_h.reshape([128, 8, 4, 128]), [8, 4, 128], 4, 0, "ps")
    bank_rr = [0]

    bank_pool = [4]

    def bank():
        b = bank_rr[0] % bank_pool[0]
        bank_rr[0] += 1
        return b

    sems = {}
    sem_names = ["pe", "act", "dve", "poolc", "ld", "x", "wg", "c", "out"] + [f"r{i}" for i in range(RING_SLOTS)]
    import contextlib
    stack = contextlib.ExitStack()
    for n in sem_names:
        sems[n] = stack.enter_context(nc.semaphore(n))

    def col(buf, i):
        return buf.v(slice(i, i + 1))

    def pvc(name, *idx):
        i, rem = pv_index(name, *idx)
        assert rem == 1
        return col(PV, i)

    def pvr(name, *idx):
        i, rem = pv_index(name, *idx)
        return PV.v(slice(i, i + rem))

    def isv(x):
        return isinstance(x, View)

    ACT_TBL = {AF.Sin: "T", AF.Tanh: "A", AF.Exp: "A", AF.Sqrt: "S", AF.Gelu_apprx_tanh: "G", AF.Silu: "L", AF.Ln: "N"}

    def dve_cost(v):
        return 0.08 + v.n / 960.0

    def ACT(out, in_, func, bias=0.0, scale=1.0, accum=None):
        reads = [in_] + [x for x in (bias, scale) if isv(x)]
        kw = dict(bias=bias.ap if isv(bias) else float(bias), scale=scale.ap if isv(scale) else float(scale))
        if accum is not None:
            kw["accum_out"] = accum.ap
        tbl = ACT_TBL.get(func)
        P.op("act", reads, [out] + ([accum] if accum is not None else []),
             lambda e: e.activation(out=out.ap, in_=in_.ap, func=func, **kw),
             dur=0.2 + out.n / 1200.0 + (0.1 if accum is not None else 0.0), tbl=tbl)

    def TT(out, a, b, op):
        P.op("dve", [a, b], [out], lambda e: e.tensor_tensor(out=out.ap, in0=a.ap, in1=b.ap, op=op), dur=dve_cost(out))

    def TS(out, a, s1, s2, op0, op1=None):
        reads = [a] + [x for x in (s1, s2) if isv(x)]
        a1 = s1.ap if isv(s1) else float(s1)
        a2 = (s2.ap if isv(s2) else (None if s2 is None else float(s2)))
        if op1 is None:
            P.op("dve", reads, [out], lambda e: e.tensor_scalar(out=out.ap, in0=a.ap, scalar1=a1, scalar2=None, op0=op0), dur=dve_cost(out))
        else:
            P.op("dve", reads, [out], lambda e: e.tensor_scalar(out=out.ap, in0=a.ap, scalar1=a1, scalar2=a2, op0=op0, op1=op1), dur=dve_cost(out))

    def STT(out, a, s, b, op0, op1):
        reads = [a, b] + ([s] if isv(s) else [])
        sv = s.ap if isv(s) else float(s)
        P.op("dve", reads, [out], lambda e: e.scalar_tensor_tensor(out=out.ap, in0=a.ap, scalar=sv, in1=b.ap, op0=op0, op1=op1), dur=dve_cost(out))

    def SCAN(out, a, b, init):
        reads = [a, b] + ([init] if isv(init) else [])
        iv = init.ap if isv(init) else float(init)
        P.op("dve", reads, [out], lambda e: e.tensor_tensor_scan(out=out.ap, data0=a.ap, data1=b.ap, initial=iv, op0=ALU.mult, op1=ALU.add), dur=0.08 + 2 * out.n / 960.0)

    def COPY(out, in_):
        P.op("dve", [in_], [out], lambda e: e.tensor_copy(out=out.ap, in_=in_.ap), dur=dve_cost(out))

    def RECIP(out, in_):
        P.op("dve", [in_], [out], lambda e: e.reciprocal(out=out.ap, in_=in_.ap), dur=dve_cost(out))

    def MEMSET(out, val):
        P.op("dve", [], [out], lambda e: e.memset(out.ap, float(val)), dur=dve_cost(out))

    def MM(out, pairs, start=True, stop=True):
        reads = [v for pr in pairs for v in pr]

        def emit(e):
            n = len(pairs)
            bi = None
            for i, (l, r) in enumerate(pairs):
                bi = e.matmul(out.ap, l.ap, r.ap, start=(start and i == 0), stop=(stop and i == n - 1))
            return bi
        P.op("pe", reads, [out], emit, dur=0.03 + sum(max(64, r.n) for _, r in pairs) / 2400.0)

    def MMI(outs_pairs, whole):
        reads = [v for (_, l, r) in outs_pairs for v in (l, r)]

        def emit(e):
            bi = None
            for o, l, r in outs_pairs:
                bi = e.matmul(o.ap, l.ap, r.ap, start=True, stop=True)
            return bi
        P.op("pe", reads, [whole], emit, dur=0.03 + sum(max(64, r.n) for _, _l, r in outs_pairs) / 2400.0)

    def ts(tt):
        return slice(tt * 512, (tt + 1) * 512)

    def sg(tt):
        return slice(2 * tt, 2 * tt + 2)

    ring_i = [0]
    slot_bufs = {}

    def wtile(src, kc, fw):
        slot = ring_i[0] % RING_SLOTS
        ring_i[0] += 1
        tot = kc * fw
        assert tot * 2 <= SLOT_BYTES
        key = (slot, kc, fw)
        if key not in slot_bufs:
            slot_bufs[key] = (sbuf([kc, fw], BF16, RING.base + slot * SLOT_BYTES), sbuf([tot], BF16, RING.base + slot * SLOT_BYTES))
        sb, flat = slot_bufs[key]
        if tot <= 2048:
            pairs = [(flat.v().ap, src)]
        else:
            assert tot % 2 == 0
            pairs = [(flat.v().ap.rearrange("p (a b) -> p a b", a=2), src.rearrange("p (a b) -> p a b", a=2))]
        P.dma("pool", sems[f"r{slot}"], pairs, [], [RING.v(slot)])
        return sb

    def sp_load(out_view, in_ap, sem="ld"):
        P.dma("sp", sems[sem], [(out_view.ap, in_ap)], [], [out_view])

    sp_load(PV.v(), d_pv)
    sp_load(MASK.v(), d_mask.rearrange("p (a b) -> p a b", b=16))
    sp_load(IDF.v(), d_ident)
    for kc in range(8):
        sp_load(X.v(kc), d_xT[kc * 128:(kc + 1) * 128, :], sem="x")
    COPY(IDENT.v(), IDF.v())
    MEMSET(ONES.v(), 1.0 / 1024.0)
    MEMSET(ONE1.v(), 1.0)
    MEMSET(ST.v(), 0.0)
    MEMSET(INIT.v(), 0.0)
    P.op("pool", [], [ROWV.v()], lambda e: e.iota(ROWV.v().ap, [[1, 16]], base=0, channel_multiplier=0, allow_small_or_imprecise_dtypes=True))
    P.op("pool", [], [COLV.v()], lambda e: e.iota(COLV.v().ap, [[1, 64]], base=0, channel_multiplier=0, allow_small_or_imprecise_dtypes=True))
    P.op("pool", [], [PIDX.v()], lambda e: e.iota(PIDX.v().ap, [[128, 2]], base=0, channel_multiplier=1, allow_small_or_imprecise_dtypes=True))
    ACT(FREQ.v(), PIDX.v(), AF.Exp, scale=-math.log(10000.0) / 256.0)
    TWO_PI = 2.0 * math.pi
    for kc in range(8):
        is_row = kc < 4
        n = 16 if is_row else 64
        src = ROWV.v() if is_row else COLV.v()
        phase = 0.0 if (kc % 4) < 2 else math.pi / 2.0
        ang, kf, ki = PANG.v(slice(0, n)), PKF.v(slice(0, n)), PKI.v(slice(0, n))
        TS(ang, src, col(FREQ, kc % 2), phase, ALU.mult, ALU.add)
        TS(kf, ang, 1.0 / TWO_PI, None, ALU.mult)
        COPY(ki, kf)
        COPY(kf, ki)
        STT(ang, kf, -TWO_PI, ang, ALU.mult, ALU.add)
        if is_row:
            flat = alias(PSNR, [16], F32).v()
            bview = PSNR.v()
        else:
            flat = alias(PSNC, [64], F32).v()
            bview = PSNC.v()
        ACT(flat, ang, AF.Sin)
        TS(flat, flat, pvc("pm", 0), None, ALU.mult)
        bb = View(bview.ap.broadcast_to([128, 16, 64]), bview.sp, bview.lo, bview.hi, 1024)
        TT(X3.v(kc, slice(0, 16)), X3.v(kc, slice(0, 16)), bb, ALU.add)
    ACT(SCF.v(), pvr("cond"), AF.Silu)

    def mod_steps(l, mb, dst):
        steps = []
        for t in range(24):
            def step(t=t):
                sl = wtile(d_mod_w[l, t], 8, 256)
                for f in range(2):
                    fc = 2 * t + f
                    MM(PS.v(mb, slice(2 * fc, 2 * fc + 2)),
                       [(sl.v(kc, slice(128 * f, 128 * f + 128)), SC.v(kc)) for kc in range(8)])
                if t % 4 == 3:
                    part = t // 4
                    off, _ = pv_index("mod_b", l)
                    for ci in range(2):
                        sel = slice(16 * part + ci, 16 * (part + 1), 2)
                        TT(dst.v(sel), PS.v(mb, sel), PV.v(slice(off + 8 * part, off + 8 * part + 8)), ALU.add)
            steps.append(step)
        return steps

    def derive(l, mod, DER, which=(0, 1, 2, 3)):
        for ci in range(2):
            def m(part):
                return mod.v(slice(2 * 8 * part + ci, 2 * 8 * (part + 1), 2))
            if 0 in which:
                STT(DER.v(0, None, ci), m(1), 1.0, pvr("norm_g", l, 0), ALU.add, ALU.mult)
            if 1 in which:
                TT(DER.v(1, None, ci), m(2), pvr("norm_g", l, 1), ALU.mult)
            if 2 in which:
                STT(DER.v(2, None, ci), m(4), 1.0, pvr("norm_g", l, 2), ALU.add, ALU.mult)
            if 3 in which:
                TT(DER.v(3, None, ci), m(5), pvr("norm_g", l, 3), ALU.mult)

    def mcol(mod, part, kc, ci):
        i = 2 * (8 * part + kc) + ci
        return col(mod, i)

    def norm_mod(der, which_a, mod, part_b, between=None):
        for tt in range(3):
            ci = 0 if tt < 2 else 1
            for kc in range(8):
                ACT(SQT.v(kc), X.v(kc, ts(tt)), AF.Square)
            MM(PS.v(5 + tt), [(ONES.v(), SQT.v(kc)) for kc in range(8)])
            ACT(RSTD.v(ts(tt)), PS.v(5 + tt), AF.Sqrt, bias=EPS)
            RECIP(RSTD.v(ts(tt)), RSTD.v(ts(tt)))
            for kc in range(8):
                t = TMP.v(kc % 2)
                TT(t, X.v(kc, ts(tt)), RSTD.v(ts(tt)), ALU.mult)
                ACT(H.v(kc, ts(tt)), t, AF.Identity, scale=der.v(which_a, slice(kc, kc + 1), ci), bias=mcol(mod, part_b, kc, ci))
            if between is not None:
                between()

    def out_proj(w2d, nk, rhs_fn, three_d, der_g, which_g, nxt, pre=None, fill=None):
        cnt_ = [0]
        side = []

        def run_side(n):
            for _ in range(n):
                if side:
                    side.pop(0)()

        def groups(tts, nside):
            P.tag = f'outproj.g{tts}'
            pend = []

            def stat(item):
                sq, fc, tt = item
                MM(PS.v(5 + tt), [(ONES.v(), sq)], start=(fc == 0), stop=(fc == 7))
            for fc in range(8):
                sl = wtile(w2d[fc], nk, 128)
                for tt in tts:
                    b = bank()
                    o = PS3.v(b) if three_d else PS.v(b)
                    MM(o, [(sl.v(kc), rhs_fn(kc, tt)) for kc in range(nk)])
                    ACT(H.v(fc, ts(tt)), PS.v(b), AF.Copy)
                    sq = SQW.v(cnt_[0] % 3)
                    cnt_[0] += 1
                    ACT(sq, PS.v(b), AF.Square)
                    pend.append((sq, fc, tt))
                    if len(pend) > 2:
                        stat(pend.pop(0))
                    run_side(nside)
            while pend:
                stat(pend.pop(0))

        def tail_steps(tt):
            ci = 0 if tt < 2 else 1

            def s0():
                P.tag = f'tail{tt}'
                ACT(RSTD.v(ts(tt)), PS.v(5 + tt), AF.Sqrt, bias=EPS)
                RECIP(RSTD.v(ts(tt)), RSTD.v(ts(tt)))

            def sk(kc):
                P.tag = f'tail{tt}'
                t = TMP.v(kc % 2)
                STT(t, H.v(kc, ts(tt)), der_g.v(which_g, slice(kc, kc + 1), ci), RSTD.v(ts(tt)), ALU.mult, ALU.mult)
                TT(X.v(kc, ts(tt)), X.v(kc, ts(tt)), t, ALU.add)
                if nxt is not None:
                    ACT(SQT.v(kc), X.v(kc, ts(tt)), AF.Square)
            return [s0] + [lambda kc=kc: sk(kc) for kc in range(8)]

        def part2_steps(tt):
            if nxt is None:
                return []
            der_n, which_a, mod_n, part_b = nxt
            ci = 0 if tt < 2 else 1

            def s0():
                P.tag = f'part2_{tt}'
                MM(PS.v(5 + tt), [(ONES.v(), SQT.v(kc)) for kc in range(8)])
                ACT(RSTD.v(ts(tt)), PS.v(5 + tt), AF.Sqrt, bias=EPS)
                RECIP(RSTD.v(ts(tt)), RSTD.v(ts(tt)))

            def sk(kc):
                P.tag = f'part2_{tt}'
                t = TMP.v(kc % 2)
                TT(t, X.v(kc, ts(tt)), RSTD.v(ts(tt)), ALU.mult)
                ACT(H.v(kc, ts(tt)), t, AF.Identity, scale=der_n.v(which_a, slice(kc, kc + 1), ci), bias=mcol(mod_n, part_b, kc, ci))
            return [s0] + [lambda kc=kc: sk(kc) for kc in range(8)]

        if pre is not None:
            for st_ in pre(0) + pre(1):
                st_()
            side.extend(pre(2))
        for tt in range(3):
            groups([tt], NSIDE)
            side.extend(tail_steps(tt))
            if tt < 2 and nxt is not None:
                side.extend([(lambda: None)] * PART2_DELAY)
            side.extend(part2_steps(tt))
        if fill is not None:
            fill()
        run_side(len(side))

    def interleave(*gens):
        gens = list(gens)
        while gens:
            for g in list(gens):
                try:
                    next(g)
                except StopIteration:
                    gens.remove(g)

    def POOL_TS(out, a, s1, op0):
        P.op("pool", [a, s1], [out], lambda e: e.tensor_scalar(out=out.ap, in0=a.ap, scalar1=s1.ap, scalar2=None, op0=op0))

    def mixer_ab(jj, der, nxt, fill):
        YA3 = ya3[0]
        for j in range(4):
            P.tag = f'gate_br{j}'
            sl = wtile(d_ab_g[jj, j], 8, 256)
            for f in range(2):
                fc = 2 * j + f
                for tt in range(3):
                    b = bank()
                    MM(PS.v(b), [(sl.v(kc, slice(128 * f, 128 * f + 128)), H.v(kc, ts(tt))) for kc in range(8)])
                    ACT(YA.v(fc, ts(tt)), PS.v(b), AF.Gelu_apprx_tanh)
        lam = pvr("lru_lam", jj)
        ACT(LC.v(0), lam, AF.Exp, scale=-1.0)
        ACT(LC.v(0), LC.v(0), AF.Ln, bias=1.0)
        TS(LC.v(1), LC.v(0), -4.0, None, ALU.mult)
        TS(BGH.v(), pvr("lru_bg", jj), 0.5, None, ALU.mult)
        TS(pvr("cf_cw", jj), pvr("cf_cw", jj), 0.5, None, ALU.mult)
        MEMSET(XBP.v(0, slice(0, 2)), 0.0)
        MEMSET(XBP.v(5, slice(258, 259)), 0.0)
        MEMSET(GP.v(None, 0, slice(0, 15)), 0.0)
        MEMSET(GP.v(None, 5, slice(271, 286)), 0.0)
        bank_pool[0] = 8

        def lru_pre(c):
            P.tag = f'lru_pre{c}'
            sl = wtile(d_ab_x[jj, c], 8, 128)
            wq = c % 2
            P.dma("pool", sems["wg"], [(WGC.v(wq).ap, d_wg[jj][:, :, c].rearrange("d g i j -> i (d g) j"))], [], [WGC.v(wq)])
            for k in range(4):
                TS(DLC.v(0, k), IDENT.v(), pvc("lru_cw", jj, k, c), None, ALU.mult)
            for tt in range(3):
                b = bank()
                MM(PS.v(b), [(sl.v(kc), H.v(kc, ts(tt))) for kc in range(8)])
                ACT(XBP.v(sg(tt), slice(2, 258)), PS3.v(b), AF.Copy)
            TT(XBP.v(slice(1, 6), slice(0, 2)), XBP.v(slice(0, 5), slice(256, 258)), MASK.v(None, slice(0, 2)), ALU.mult)
            TT(XBP.v(slice(0, 5), slice(258, 259)), XBP.v(slice(1, 6), slice(2, 3)), MASK.v(None, slice(0, 1)), ALU.mult)
            for tt in range(3):
                b = bank()
                MM(PS3.v(b), [(DLC.v(0, k), XBP.v(sg(tt), slice(k, k + 256))) for k in range(4)])
                ACT(XCB2.v(wq, ts(tt)), PS.v(b), AF.Identity, bias=pvc("lru_cb", jj, c))

        def lru_chunk(c):
            wq = c % 2
            if c == 0:
                lru_pre(0)
                yield
            for d in range(2):
                order = [0, 1, 2] if d == 0 else [2, 1, 0]
                chc = LC.v(1, slice(d * 8 + c, d * 8 + c + 1))
                if d == 1 and c < 7:
                    lru_pre(c + 1)
                P.tag = f'lru{c}.d{d}s1'
                for q, tt in enumerate(order):
                    Aa, TI, S = (LT.v(q, z) for z in range(3))
                    br = bank()
                    MM(PS.v(br), [(WGC.v(wq, d * 2 + 0), XCB2.v(wq, ts(tt)))])
                    bi_ = bank()
                    MM(PS.v(bi_), [(WGC.v(wq, d * 2 + 1), XCB2.v(wq, ts(tt)))])
                    ACT(Aa, PS.v(br), AF.Tanh, scale=0.5, bias=col(BGH, (d * 2 + 0) * 8 + c))
                    ACT(TI, PS.v(bi_), AF.Tanh, scale=0.5, bias=col(BGH, (d * 2 + 1) * 8 + c))
                    ACT(Aa, Aa, AF.Exp, scale=chc, bias=chc)
                    STT(S, Aa, -1.0, Aa, ALU.mult, ALU.mult)
                    TS(S, S, 1.0, 0.0, ALU.add, ALU.max)
                    STT(TI, TI, 1.0, XCB2.v(wq, ts(tt)), ALU.add, ALU.mult)
                    if d == 0:
                        TT(LT.v(q, 0, slice(256, 257)), LT.v(q, 0, slice(256, 257)), MASK.v(2 * tt, slice(0, 1)), ALU.mult)
                        if tt == 1:
                            TT(LT.v(q, 0, slice(0, 1)), LT.v(q, 0, slice(0, 1)), MASK.v(1, slice(0, 1)), ALU.mult)
                    else:
                        TT(LT.v(q, 0, slice(255, 256)), LT.v(q, 0, slice(255, 256)), MASK.v(2 * tt, slice(0, 1)), ALU.mult)
                        if tt == 0:
                            TT(LT.v(q, 0, slice(511, 512)), LT.v(q, 0, slice(511, 512)), MASK.v(1, slice(0, 1)), ALU.mult)
                yield
                P.tag = f'lru{c}.d{d}s2'
                for q, tt in enumerate(order):
                    Aa, TI, S = (LT.v(q, z) for z in range(3))
                    ACT(S, S, AF.Sqrt, scale=0.25)
                for q, tt in enumerate(order):
                    Aa, TI, S = (LT.v(q, z) for z in range(3))
                    TT(TI, TI, S, ALU.mult)
                    r0 = (jj * 2 + d) * 6
                    if d == 0:
                        init = pvc("h0", jj, 0, c) if tt == 0 else (HF.v(slice(511, 512)) if tt == 1 else 0.0)
                        SCAN(HF.v(ts(tt)), Aa, TI, init)
                        if tt == 2:
                            COPY(ST.v(c, slice(r0, r0 + 6)), HF.v(slice(255, NT, 256)))
                    else:
                        hb = HB.v(q % 2)
                        init = 0.0 if tt == 2 else (pvc("h0", jj, 1, c) if tt == 1 else ST.v(c, slice(r0 + 2, r0 + 3)))
                        SCAN(rev(hb), rev(Aa), rev(TI), init)
                        COPY(ST.v(c, slice(r0 + 2 * tt, r0 + 2 * tt + 2)), HB.v(q % 2, slice(0, 512, 256)))
                        TT(hb, hb, HF.v(ts(tt)), ALU.add)
                        TT(YA.v(c, ts(tt)), YA.v(c, ts(tt)), hb, ALU.mult)
                yield

        def dconf_builds(c):
            for k in range(31):
                if k % 2 == 0:
                    ACT(DCONF.v(k), IDENT.v(), AF.Identity, scale=pvc("cf_cw", jj, k, c))
                else:
                    TS(DCONF.v(k), IDENT.v(), pvc("cf_cw", jj, k, c), None, ALU.mult)

        def conf_chunk(c):
            P.tag = f'conf{c}.glub'
            sl = wtile(d_ab_glu[jj, c], 8, 256)
            if c == 0:
                dconf_builds(0)
            for tt in range(3):
                b = bank()
                MM(PS.v(b), [(sl.v(kc, slice(128, 256)), H.v(kc, ts(tt))) for kc in range(8)])
                ACT(TG.v(ts(tt)), PS.v(b), AF.Tanh, scale=0.5)
            P.tag = f'conf{c}.glua'
            for tt in range(3):
                b = bank()
                MM(PS.v(b), [(sl.v(kc, slice(0, 128)), H.v(kc, ts(tt))) for kc in range(8)])
                STT(GP.v(c, sg(tt), slice(15, 271)), TG3.v(sg(tt)), 1.0, PS3.v(b), ALU.add, ALU.mult)
            TT(GP.v(c, slice(1, 6), slice(0, 15)), GP.v(c, slice(0, 5), slice(256, 271)), MASK.v(None, slice(0, 15)), ALU.mult)
            TT(GP.v(c, slice(0, 5), slice(271, 286)), GP.v(c, slice(1, 6), slice(15, 30)), MASK.v(None, slice(0, 15)), ALU.mult)
            yield
            P.tag = f'conf{c}.conv'
            for tt in range(3):
                b = bank()
                MM(PS3.v(b), [(DCONF.v(k), GP.v(c, sg(tt), slice(k, k + 256))) for k in range(31)])
                ACT(GP.v(c, sg(tt), slice(15, 271)), PS3.v(b), AF.Identity, bias=pvc("cf_cb", jj, c))
                if tt == 2 and c < 7:
                    P.tag = f'conf{c}.builds'
                    dconf_builds(c + 1)
                yield

        for c in range(8):
            interleave(lru_chunk(c), conf_chunk(c))
        bank_pool[0] = 4

        def ln_tile(tt):
            def s0():
                for c in range(8):
                    ACT(SQT3.v(c), GP.v(c, sg(tt), slice(15, 271)), AF.Square)
                bm = bank()
                MM(PS3.v(bm), [(ONES.v(), GP.v(c, sg(tt), slice(15, 271))) for c in range(8)])
                be = bank()
                MM(PS.v(be), [(ONES.v(), SQT.v(c)) for c in range(8)])
                ACT(LNT.v(0), PS.v(bm), AF.Copy)
                TT(LNT.v(1), LNT.v(0), LNT.v(0), ALU.mult)
                TT(LNT.v(1), PS.v(be), LNT.v(1), ALU.subtract)
                TS(LNT.v(1), LNT.v(1), 0.0, None, ALU.max)
                ACT(LNT.v(1), LNT.v(1), AF.Sqrt, bias=EPS)
                RECIP(LNT.v(1), LNT.v(1))
                TT(LNT.v(2), LNT.v(0), LNT.v(1), ALU.mult)

            def sk(c):
                t = TMP3.v(c % 2)
                g = GP.v(c, sg(tt), slice(15, 271))
                TT(t, g, LNT3.v(1), ALU.mult)
                TT(t, t, LNT3.v(2), ALU.subtract)
                ACT(g, t, AF.Silu, scale=pvc("cf_g", jj, c), bias=pvc("cf_b", jj, c))
            return [s0] + [lambda c=c: sk(c) for c in range(8)]

        def rhs3(kc, tt):
            return YA3.v(kc, sg(tt)) if kc < 8 else GP.v(kc - 8, sg(tt), slice(15, 271))
        out_proj(d_ab_out[jj], 16, rhs3, True, der, 1, nxt, pre=ln_tile, fill=fill)

    ya3 = [alias(YA, [8, 6, 256], BF16)]

    def mixer_c(jj, der, nxt, fill):
        P.dma("pool", sems["wg"], [(WST.v().ap, d_wst[jj])], [], [WST.v()])
        MEMSET(LB.v(parts=slice(0, 2)), 1.0)
        P.dma("sp", sems["c"], [(LB.v(parts=slice(0, 1)).ap, d_clnb[jj:jj + 1, :].rearrange("o (a b) -> o a b", b=128))], [], [LB.v()])
        P.dma("sp", sems["c"], [(RB.v(parts=slice(1, 2)).ap, d_cbs[jj:jj + 1, :, :])], [], [RB.v()])
        for sI in range(8):
            P.tag = f'c_v{sI}'
            sl = wtile(d_c_v[jj, sI], 8, 256)
            for m in range(12):
                b = bank()
                MM(PS.v(b, slice(0, 256)), [(H.v(kc, slice(128 * m, 128 * m + 128)), sl.v(kc)) for kc in range(8)])
                vv = V.v(slice(2 * sI, 2 * sI + 2), m)
                ACT(vv, PS4.v(b, slice(0, 2)), AF.Gelu_apprx_tanh, accum=CS1.v(m, slice(sI, sI + 1)))
                jf = JUNKF.v((sI * 12 + m) % 2)
                TT(JUNKF3.v((sI * 12 + m) % 2), vv, vv, ALU.mult)
                P.op("dve", [jf], [CS2.v(m, slice(sI, sI + 1))], lambda e, jf=jf, o=CS2.v(m, slice(sI, sI + 1)): e.reduce_sum(out=o.ap, in_=jf.ap, axis=AX.X), dur=0.35)
        P.op("dve", [CS1.v()], [CST.v(0)], lambda e: e.reduce_sum(out=CST.v(0).ap, in_=CS1.v().ap, axis=AX.X))
        P.op("dve", [CS2.v()], [CST.v(1)], lambda e: e.reduce_sum(out=CST.v(1).ap, in_=CS2.v().ap, axis=AX.X))
        TS(CST.v(2), CST.v(0), 1.0 / 2048.0, None, ALU.mult)
        TT(CST.v(3), CST.v(2), CST.v(2), ALU.mult)
        STT(CST.v(4), CST.v(1), 1.0 / 2048.0, CST.v(3), ALU.mult, ALU.subtract)
        TS(CST.v(4), CST.v(4), 0.0, None, ALU.max)
        ACT(CST.v(4), CST.v(4), AF.Sqrt, bias=EPS)
        RECIP(CST.v(4), CST.v(4))
        STT(CST.v(5), CST.v(2), -1.0, CST.v(4), ALU.mult, ALU.mult)
        for m in range(12):
            if m % 2 == 0:
                TS(V.v(None, m), V.v(None, m), CST.v(4, slice(m, m + 1)), CST.v(5, slice(m, m + 1)), ALU.mult, ALU.add)
            else:
                ACT(V.v(None, m), V.v(None, m), AF.Identity, scale=CST.v(4, slice(m, m + 1)), bias=CST.v(5, slice(m, m + 1)))
        for h in range(8):
            b = bank()
            MM(PS.v(b, slice(0, 128), parts=slice(0, 1)), [(ONE1.v(slice(0, 1)), WST.v(h))])
            ACT(RB.v(h, parts=slice(0, 1)), PS.v(b, slice(0, 128), parts=slice(0, 1)), AF.Copy)
        for dc in range(16):
            b = bank()
            MM(PS.v(b, slice(0, 128)), [(LB.v(dc, parts=slice(0, 2)), RB.v(dc // 2, parts=slice(0, 2)))])
            ACT(BT.v(dc, 0), PS.v(b, slice(0, 128)), AF.Copy)
        i = 0
        for h in range(8):
            P.tag = f'c_u{h}'
            sl = wtile(d_c_u[jj, h], 8, 256)
            for f in range(2):
                dc = 2 * h + f
                for tt in range(3):
                    b = bank()
                    MMI([(PS.v(b, slice(128 * n, 128 * n + 128)), V.v(dc, 4 * tt + n), WST.v(h)) for n in range(4)], PS.v(b))
                    s1 = S1T4.v(i % 3)
                    bt = BT.v(dc)
                    btb = View(bt.ap.broadcast_to([128, 4, 128]), bt.sp, bt.lo, bt.hi, 512)
                    STT(s1, PS4.v(b), pvc("c_lng", jj, dc), btb, ALU.mult, ALU.add)
                    b2 = bank()
                    MM(PS.v(b2), [(sl.v(kc, slice(128 * f, 128 * f + 128)), H.v(kc, ts(tt))) for kc in range(8)])
                    ug = UG.v(i % 3)
                    ACT(ug, PS.v(b2), AF.Gelu_apprx_tanh)
                    TT(US.v(dc, ts(tt)), ug, S1T.v(i % 3), ALU.mult)
                    i += 1
        out_proj(d_c_out[jj], 16, lambda kc, tt: US.v(kc, ts(tt)), False, der, 1, nxt, fill=fill)

    def ffn(l, extra_steps, der, nxt_fn):
        for Z in (ZG, ZV):
            MEMSET(Z.v(0, slice(0, 1)), 0.0)
            MEMSET(Z.v(5, slice(257, 258)), 0.0)
        for j in range(22):
            P.tag = f'ffn_up{j}'
            sl = wtile(d_up[l, j], 8, 256)
            ds_ = j % 2
            for k in range(3):
                TS(DF.v(ds_, k), IDENT.v(), pvc("f_cw", l, k, j), None, ALU.mult)
            for tt in range(3):
                b = bank()
                MM(PS.v(b), [(sl.v(kc, slice(0, 128)), H.v(kc, ts(tt))) for kc in range(8)])
                ACT(ZG.v(sg(tt), slice(1, 257)), PS3.v(b), AF.Copy)
            TT(ZG.v(slice(1, 6), slice(0, 1)), ZG.v(slice(0, 5), slice(256, 257)), MASK.v(None, slice(0, 1)), ALU.mult)
            TT(ZG.v(slice(0, 5), slice(257, 258)), ZG.v(slice(1, 6), slice(1, 2)), MASK.v(None, slice(0, 1)), ALU.mult)
            for tt in range(3):
                b = bank()
                MM(PS.v(b), [(sl.v(kc, slice(128, 256)), H.v(kc, ts(tt))) for kc in range(8)])
                ACT(ZV.v(sg(tt), slice(1, 257)), PS3.v(b), AF.Copy)
                ACT(OT3.v(tt), PS3.v(b), AF.Identity, scale=pvc("f_cw", l, 1, 22 + j), bias=pvc("f_cb", l, 22 + j))
            TT(ZV.v(slice(1, 6), slice(0, 1)), ZV.v(slice(0, 5), slice(256, 257)), MASK.v(None, slice(0, 1)), ALU.mult)
            TT(ZV.v(slice(0, 5), slice(257, 258)), ZV.v(slice(1, 6), slice(1, 2)), MASK.v(None, slice(0, 1)), ALU.mult)
            for tt in range(3):
                b = bank()
                MM(PS3.v(b), [(DF.v(ds_, k), ZG.v(sg(tt), slice(k, k + 256))) for k in range(3)])
                ACT(GT.v(tt), PS.v(b), AF.Gelu_apprx_tanh, bias=pvc("f_cb", l, j))
            for tt in range(3):
                o = OT3.v(tt)
                STT(o, ZV.v(sg(tt), slice(0, 256)), pvc("f_cw", l, 0, 22 + j), o, ALU.mult, ALU.add)
                STT(o, ZV.v(sg(tt), slice(2, 258)), pvc("f_cw", l, 2, 22 + j), o, ALU.mult, ALU.add)
                TT(U.v(j, ts(tt)), OT.v(tt), GT.v(tt), ALU.mult)
            if extra_steps and j % 4 != 3:
                extra_steps.pop(0)()
        while extra_steps:
            extra_steps.pop(0)()
        out_proj(d_down[l], 22, lambda kc, tt: U.v(kc, ts(tt)), False, der, 3, nxt_fn())

    steps0 = mod_steps(0, 4, MODS[0])
    for _ in range(8):
        steps0.pop(0)()
    derive(0, MODS[0], DERS[0], which=(0,))

    def between0():
        for _ in range(6):
            if steps0:
                steps0.pop(0)()
    norm_mod(DERS[0], 0, MODS[0], 0, between=between0)
    while steps0:
        steps0.pop(0)()
    derive(0, MODS[0], DERS[0], which=(1, 2, 3))
    for l in range(n_layers):
        mod = MODS[l % 2]
        der = DERS[l % 2]
        last = (l + 1 == n_layers)
        steps = mod_steps(l + 1, 4, MODS[(l + 1) % 2]) if not last else []

        def fill(steps=steps):
            for _ in range(8):
                if steps:
                    steps.pop(0)()
        if l % 2 == 0:
            mixer_ab(l // 2, der, (der, 2, mod, 3), fill)
        else:
            mixer_c(l // 2, der, (der, 2, mod, 3), fill)

        def nxt_fn(l=l, last=last):
            if last:
                return None
            derive(l + 1, MODS[(l + 1) % 2], DERS[(l + 1) % 2])
            return (DERS[(l + 1) % 2], 0, MODS[(l + 1) % 2], 0)
        ffn(l, steps, der, nxt_fn)

    for kc in range(8):
        P.dma("sp", sems["out"], [(d_yT[kc * 128:(kc + 1) * 128, :], X.v(kc).ap)], [X.v(kc)], [])
    for c in range(8):
        b = bank()
        o = PS.v(b, slice(0, 128), parts=slice(0, 24))
        P.op("pe", [ST.v(c), IDF.v()], [o], lambda e, o=o, c=c: e.transpose(out=o.ap, in_=ST.v(c).ap, identity=IDF.v().ap))
        ACT(STT_.v(slice(128 * c, 128 * c + 128), parts=slice(0, 24)), o, AF.Copy)
    P.dma("sp", sems["out"], [(d_st, STT_.v(parts=slice(0, 24)).ap)], [STT_.v()], [])
    fin = P.op("sp", [], [], lambda e: None)
    for ins in P.q["sp"]:
        if ins.dma_sem is sems["out"]:
            fin.deps.add(ins)

    for sname in ("ld", "x"):
        tot = 16 * P.dma_count.get(sems[sname], 0)
        for ins in P.q["sp"]:
            if ins.dma_sem is sems[sname]:
                ins.dma_val = tot

    if MODEL_REPORT and not SCHEDULE:
        print(f"[kernel] {len(P.all)} instructions; model makespan {P.simulate():.0f} us", flush=True)
    if SCHEDULE:
        base = P.simulate()
        est = P.schedule()
        chk = P.simulate()
        print(f"[kernel] {len(P.all)} instructions; model makespan recorded order {base:.0f} us -> list-scheduled {est:.0f} us (in-order replay {chk:.0f} us)", flush=True)

    eng_sem = {"pe": sems["pe"], "act": sems["act"], "dve": sems["dve"], "pool": sems["poolc"]}
    for e in ("pe", "act", "dve", "pool"):
        n = 0
        for ins in P.q[e]:
            if ins.flag and ins.dma_sem is None:
                n += 1
                ins.ms = n

    def run_queue(ename, eng):
        known = {}
        for ins in P.q[ename]:
            need = {}
            for d in ins.deps:
                if d.dma_sem is not None:
                    s, v = d.dma_sem, d.dma_val
                else:
                    s, v = eng_sem[d.eng], d.ms
                    assert v > 0
                if need.get(id(s), (None, 0))[1] < v:
                    need[id(s)] = (s, v)
            for sid, (s, v) in need.items():
                if known.get(sid, 0) >= v:
                    continue
                eng.wait_ge(s, v)
                known[sid] = v
            bi = ins.emit(eng)
            if ins.flag and ins.dma_sem is None:
                assert bi is not None
                bi.then_inc(eng_sem[ename], 1)

    with stack:
        with nc.Block() as block:
            @block.tensor
            def _(e):
                run_queue("pe", e)

            @block.scalar
            def _(e):
                run_queue("act", e)

            @block.vector
            def _(e):
                run_queue("dve", e)

            @block.gpsimd
            def _(e):
                run_queue("pool", e)

            @block.sync
            def _(e):
                run_queue("sp", e)
    return nc


def ffn_mod_bank_guard(steps):
    return steps


def to_fm(v):
    v = np.asarray(v, np.float32)
    lead = v.shape[:-1]
    n = v.shape[-1] // 128
    return np.ascontiguousarray(np.moveaxis(v.reshape(*lead, n, 128), -1, 0))


_NC_CACHE = {}


def kernel(x_prompt, x_sample, state_lru, c, c_ctx, mod_w, mod_b, norm_g,
           ab_w_in, lru_conv_w, lru_conv_b, lru_w_gates, lru_b_gates, lru_lambda,
           conf_conv_w, conf_conv_b, conf_ln_g, conf_ln_b, ab_w_out,
           c_w_in, c_ln_g, c_ln_b, c_w_s, c_b_s, c_w_out,
           ffn_w_up, ffn_conv_w, ffn_conv_b, ffn_w_down, _n_layers=4):
    f = lambda a: np.ascontiguousarray(np.asarray(a, np.float32))
    x_prompt, x_sample, state_lru, c, c_ctx = map(f, (x_prompt, x_sample, state_lru, c, c_ctx))
    shared = {
        "norm_g": to_fm(norm_g), "mod_b": to_fm(mod_b), "lru_cw": to_fm(lru_conv_w), "lru_cb": to_fm(lru_conv_b),
        "lru_bg": to_fm(lru_b_gates), "lru_lam": to_fm(lru_lambda), "cf_cw": to_fm(conf_conv_w),
        "cf_cb": to_fm(conf_conv_b), "cf_g": to_fm(conf_ln_g), "cf_b": to_fm(conf_ln_b),
        "c_lng": to_fm(c_ln_g), "f_cw": to_fm(ffn_conv_w), "f_cb": to_fm(ffn_conv_b),
    }
    ident = np.eye(128, dtype=np.float32)
    wst = np.ascontiguousarray(np.transpose(f(c_w_s), (0, 3, 1, 2)))
    def tile_cols(w, col_groups):
        L, K, _ = w.shape
        kc = K // 128
        tiles = []
        for groups_ in col_groups:
            blk = np.concatenate([w[:, :, a:b] for a, b in groups_], axis=2)
            fw = blk.shape[2]
            tiles.append(blk.reshape(L, kc, 128, fw).transpose(0, 2, 1, 3).reshape(L, 128, kc * fw))
        return np.ascontiguousarray(np.stack(tiles, axis=1))

    ab_in, c_in, up_w = f(ab_w_in), f(c_w_in), f(ffn_w_up)
    common = {
        "ident": ident,
        "mod_w": tile_cols(f(mod_w), [[(256 * t, 256 * t + 256)] for t in range(24)]),
        "ab_g": tile_cols(ab_in, [[(256 * j, 256 * j + 256)] for j in range(4)]),
        "ab_x": tile_cols(ab_in, [[(1024 + 128 * c_, 1024 + 128 * c_ + 128)] for c_ in range(8)]),
        "ab_glu": tile_cols(ab_in, [[(2048 + 128 * c_, 2048 + 128 * c_ + 128), (3072 + 128 * c_, 3072 + 128 * c_ + 128)] for c_ in range(8)]),
        "ab_w_out": tile_cols(f(ab_w_out), [[(128 * q, 128 * q + 128)] for q in range(8)]),
        "c_v": tile_cols(c_in, [[(2048 + 256 * q, 2048 + 256 * q + 256)] for q in range(8)]),
        "c_u": tile_cols(c_in, [[(256 * q, 256 * q + 256)] for q in range(8)]),
        "c_w_out": tile_cols(f(c_w_out), [[(128 * q, 128 * q + 128)] for q in range(8)]),
        "ffn_w_up": tile_cols(up_w, [[(128 * q, 128 * q + 128), (2816 + 128 * q, 2816 + 128 * q + 128)] for q in range(22)]),
        "ffn_w_down": tile_cols(f(ffn_w_down), [[(128 * q, 128 * q + 128)] for q in range(8)]),
        "lru_wg": f(lru_w_gates), "wst": wst, "c_bs": f(c_b_s), "c_lnb": f(c_ln_b),
    }
    prompts_of = []
    in_maps = []
    for core in range(8):
        if core < 4:
            pr = [2 * core, 2 * core + 1]
            xs = np.concatenate([x_sample[core]] + [x_prompt[p] for p in pr], axis=0)
            condS = c[core]
            h0 = state_lru[core]
            m = [1.0, 1.0, 1.0, 0.0, 0.0]
            pm = 1.0
        else:
            pr = list(range(8 + 6 * (core - 4), 8 + 6 * (core - 4) + 6))
            xs = np.concatenate([x_prompt[p] for p in pr], axis=0)
            condS = c_ctx
            h0 = np.zeros((2, 2, 1024), np.float32)
            m = [0.0] * 5
            pm = 0.0
        prompts_of.append(pr)
        pvd = dict(shared)
        pvd["h0"] = to_fm(h0)
        pvd["cond"] = np.ascontiguousarray(np.moveaxis(to_fm(np.stack([condS, c_ctx], 0)), 1, 2))
        pvd["pm"] = np.full((128, 1), pm, np.float32)
        pv = np.concatenate([pvd[n].reshape(128, -1) for n, _ in PV_SPEC], axis=1).astype(np.float32)
        assert pv.shape == (128, NPV), pv.shape
        mask = np.ascontiguousarray(np.broadcast_to(np.asarray(m, np.float32)[None, :, None], (128, 5, 16)).reshape(128, 80))
        d = dict(common)
        d.update({"xT": np.ascontiguousarray(xs.T), "pv": pv, "mask": mask})
        in_maps.append(d)
    if _n_layers not in _NC_CACHE:
        _NC_CACHE[_n_layers] = build(_n_layers)
    nc = _NC_CACHE[_n_layers]
    res = run_bass_kernel_spmd(nc, in_maps, core_ids=list(range(8)))
    y_prompt = np.zeros((32, 256, 1024), np.float32)
    y_sample = np.zeros((4, 1024, 1024), np.float32)
    new_state = np.zeros((32, 2, 2, 1024), np.float32)
    for core in range(8):
        y = np.asarray(res.results[core]["yT"], np.float32).T
        st = np.asarray(res.results[core]["st"], np.float32).reshape(2, 2, 6, 1024)
        if core < 4:
            y_sample[core] = y[:1024]
            segs = [4, 5]
        else:
            segs = list(range(6))
        for s, p in zip(segs, prompts_of[core]):
            y_prompt[p] = y[s * 256:(s + 1) * 256]
            new_state[p] = st[:, :, s, :]
    return (y_prompt, y_sample, new_state)
```

```python
import math
import numpy as np
import concourse.bass as bass
import concourse.mybir as mybir
from concourse.bass_utils import run_bass_kernel_spmd

F32 = mybir.dt.float32
I32 = mybir.dt.int32
BF16 = mybir.dt.bfloat16
AF = mybir.ActivationFunctionType
ALU = mybir.AluOpType
AX = mybir.AxisListType

NT = 1536
NSEG = 6
EPS = 1e-6
SB_BASE = 16512
SB_END = 229376
RING_SLOTS = 3
SCHEDULE = False
MODEL_REPORT = False
SCHED_WINDOW = 2000
PART2_DELAY = 4
D2_FFN = 11
D2_AB = 8
D2_C = 6
NSIDE = 3
SLOT_BYTES = 5632

PV_SPEC = [
    ("norm_g", (4, 4, 8)), ("mod_b", (4, 48)), ("lru_cw", (2, 4, 8)), ("lru_cb", (2, 8)),
    ("lru_bg", (2, 2, 2, 8)), ("lru_lam", (2, 2, 8)), ("cf_cw", (2, 31, 8)), ("cf_cb", (2, 8)),
    ("cf_g", (2, 8)), ("cf_b", (2, 8)), ("c_lng", (2, 16)), ("f_cw", (4, 3, 44)), ("f_cb", (4, 44)),
    ("h0", (2, 2, 8)), ("cond", (8, 2)), ("pm", (1,)),
]
PV_OFF = {}
_o = 0
for _n, _s in PV_SPEC:
    PV_OFF[_n] = (_o, _s)
    _o += int(np.prod(_s))
NPV = _o


def pv_index(name, *idx):
    off, shp = PV_OFF[name]
    flat = 0
    for i, n in zip(idx, shp):
        flat = flat * n + i
    rem = int(np.prod(shp[len(idx):])) if len(idx) < len(shp) else 1
    return off + flat * rem, rem


class View:
    __slots__ = ("ap", "sp", "lo", "hi", "n")

    def __init__(self, ap, sp, lo, hi, n=512):
        self.ap, self.sp, self.lo, self.hi, self.n = ap, sp, lo, hi, n


def rev(v):
    return View(v.ap[:, ::-1], v.sp, v.lo, v.hi, v.n)


class Buf:
    def __init__(self, handle, shape, esz, base, sp):
        self.t = handle
        self.shape = list(shape)
        self.esz = esz
        self.base = base
        self.sp = sp
        st = []
        s = 1
        for n in reversed(self.shape):
            st.append(s)
            s *= n
        self.strides = list(reversed(st))
        self.nbytes = s * esz

    def v(self, *idx, parts=None):
        key = [slice(None) if parts is None else parts]
        lo = 0
        hi = 0
        cnt = 1
        for d, (n, st) in enumerate(zip(self.shape, self.strides)):
            i = idx[d] if d < len(idx) else None
            if i is None:
                i = slice(None)
            if isinstance(i, int):
                a = b = i
            else:
                a, b, step = i.indices(n)
                assert step > 0 and b > a, (i, n)
                cnt *= (b - a - 1) // step + 1
                b = a + ((b - a - 1) // step) * step
            key.append(i)
            lo += a * st
            hi += b * st
        blo = self.base + lo * self.esz
        bhi = self.base + (hi + 1) * self.esz
        if self.sp == "ps":
            blo = blo // 2048 * 2048
            bhi = (bhi + 2047) // 2048 * 2048
        return View(self.t.ap()[tuple(key)], self.sp, blo, bhi, cnt)


class Ins:
    __slots__ = ("eng", "emit", "deps", "order", "flag", "ms", "dma_sem", "dma_val", "key", "gi", "dur", "tbl", "lat", "tag")

    def __init__(self, eng, emit):
        self.eng = eng
        self.emit = emit
        self.deps = set()
        self.order = set()
        self.flag = False
        self.ms = 0
        self.dma_sem = None
        self.dma_val = 0
        self.key = eng
        self.gi = 0
        self.dur = 0.3
        self.tbl = None
        self.lat = 0.0


class Prog:
    ENGS = ("pe", "act", "dve", "pool", "sp")

    def __init__(self):
        self.q = {e: [] for e in self.ENGS}
        self.recs = {"sb": {}, "ps": {}}
        self.dma_count = {}
        self.all = []
        self.tag = ''

    def _track(self, ins, reads, writes):
        seen = set()
        for v, w in [(r, False) for r in reads] + [(x, True) for x in writes]:
            k = (v.sp, v.lo, v.hi, w)
            if k in seen:
                continue
            seen.add(k)
            for key, tok in self.recs[v.sp].items():
                if key[0] < v.hi and v.lo < key[1] and (key[2] or w):
                    if tok is ins:
                        continue
                    if tok.eng == "pe" and ins.eng == "pe" and tok.dma_sem is None:
                        ins.order.add(tok)
                        continue
                    ins.deps.add(tok)
                    tok.flag = True
        for v in writes:
            recs = self.recs[v.sp]
            dead = [k for k in recs if k[0] >= v.lo and k[1] <= v.hi]
            for k in dead:
                del recs[k]
            recs[(v.lo, v.hi, True, ins.key)] = ins
        for v in reads:
            k = (v.lo, v.hi, False, ins.key)
            old = self.recs[v.sp].get(k)
            if old is not None and old is not ins:
                ins.order.add(old)
            self.recs[v.sp][k] = ins

    def _add(self, ins):
        ins.gi = len(self.all)
        ins.tag = self.tag
        self.all.append(ins)
        self.q[ins.eng].append(ins)

    def op(self, eng, reads, writes, emit, dur=0.3, tbl=None):
        ins = Ins(eng, emit)
        ins.dur = dur
        ins.tbl = tbl
        self._track(ins, reads, writes)
        self._add(ins)
        return ins

    def dma(self, queue, sem, pairs, reads, writes):
        n = self.dma_count.get(sem, 0) + len(pairs)
        self.dma_count[sem] = n

        def emit(e, pairs=pairs, sem=sem):
            bi = None
            for o, i in pairs:
                bi = e.dma_start(out=o, in_=i)
                bi.then_inc(sem, 16)
            return None

        ins = Ins(queue, emit)
        ins.dma_sem = sem
        ins.dma_val = 16 * n
        ins.key = ("dma", id(sem))
        nel = sum(v.n for v in (writes or reads))
        ins.dur = 0.6
        ins.lat = 2.0 + nel * 512 / 300e3
        self._track(ins, reads, writes)
        self._add(ins)
        return ins

    def schedule(self):
        import heapq
        N = len(self.all)
        preds = [set() for _ in range(N)]
        for ins in self.all:
            for d in ins.deps:
                preds[ins.gi].add(d.gi)
            for d in ins.order:
                preds[ins.gi].add(d.gi)
        for e in ("pool", "sp"):
            qq = self.q[e]
            for a_, b_ in zip(qq, qq[1:]):
                preds[b_.gi].add(a_.gi)
        succ = [[] for _ in range(N)]
        indeg = [0] * N
        for i in range(N):
            indeg[i] = len(preds[i])
            for p in preds[i]:
                succ[p].append(i)
        blevel = [0.0] * N
        for i in range(N - 1, -1, -1):
            ins = self.all[i]
            m = 0.0
            for sidx in succ[i]:
                if blevel[sidx] > m:
                    m = blevel[sidx]
            blevel[i] = ins.dur + ins.lat + 0.12 + m
        ready_t = [0.0] * N
        ready = {e: [] for e in self.ENGS}
        for ins in self.all:
            if indeg[ins.gi] == 0:
                ready[ins.eng].append(ins.gi)
        eng_free = {e: 0.0 for e in self.ENGS}
        cur_tbl = [None]
        newq = {e: [] for e in self.ENGS}
        done = 0
        WINDOW = SCHED_WINDOW
        low = 0
        sched = [False] * N
        while done < N:
            best = None
            for e in self.ENGS:
                r = ready[e]
                if not r:
                    continue
                tmin = min(ready_t[g] for g in r)
                t_e = max(eng_free[e], tmin)
                pick = None
                for g in r:
                    if ready_t[g] <= t_e + 1e-9 and g <= low + WINDOW:
                        key = (-blevel[g], g)
                        if pick is None or key < pick[0]:
                            pick = (key, g)
                if pick is None:
                    g = min(r)
                    t_e = max(eng_free[e], ready_t[g])
                    pick = (None, g)
                cand = (t_e, pick[1], e)
                if best is None or cand < best:
                    best = cand
            st, gi, e = best
            ready[e].remove(gi)
            ins = self.all[gi]
            if e == "act" and ins.tbl is not None and ins.tbl != cur_tbl[0]:
                st += 1.3
                cur_tbl[0] = ins.tbl
            fin = st + ins.dur
            eng_free[e] = fin
            newq[e].append(ins)
            sched[gi] = True
            while low < N and sched[low]:
                low += 1
            done += 1
            avail = fin + ins.lat + 0.12
            for sidx in succ[gi]:
                if ready_t[sidx] < avail:
                    ready_t[sidx] = avail
                indeg[sidx] -= 1
                if indeg[sidx] == 0:
                    ready[self.all[sidx].eng].append(sidx)
        self.q = newq
        return max(eng_free.values())

    def simulate(self, report=None):
        fin_t = {}
        ptr = {e: 0 for e in self.ENGS}
        eng_free = {e: 0.0 for e in self.ENGS}
        cur_tbl = None
        left = sum(len(q) for q in self.q.values())
        while left:
            progressed = False
            for e in self.ENGS:
                q = self.q[e]
                while ptr[e] < len(q):
                    ins = q[ptr[e]]
                    pr = list(ins.deps) + list(ins.order)
                    if any(p.gi not in fin_t for p in pr):
                        break
                    st = max([eng_free[e]] + [fin_t[p.gi] + p.lat + 0.12 for p in pr])
                    if report is not None and e == "pe" and st - eng_free[e] > report[2] and report[0] <= st <= report[1]:
                        cp = max(pr, key=lambda p: fin_t[p.gi] + p.lat)
                        print(f"  PE idle {eng_free[e]:8.1f}->{st:8.1f} [{ins.tag}] waits on {cp.eng}:{cp.tag} (dur {cp.dur:.2f}, dma={cp.dma_sem is not None})")
                    if e == "act" and ins.tbl is not None and ins.tbl != cur_tbl:
                        st += 1.3
                        cur_tbl = ins.tbl
                    fin_t[ins.gi] = st + ins.dur
                    eng_free[e] = st + ins.dur
                    ptr[e] += 1
                    left -= 1
                    progressed = True
            assert progressed, "deadlock in queue order"
        return max(eng_free.values())


def build(n_layers=4):
    nc = bass.Bass("TRN2", target_bir_lowering=False)

    def dr(name, shape, kind="ExternalInput"):
        return nc.dram_tensor(name, list(shape), F32, kind=kind).ap()

    d_xT = dr("xT", [1024, NT])
    d_pv = dr("pv", [128, NPV])
    d_mask = dr("mask", [128, 80])
    d_ident = dr("ident", [128, 128])
    d_mod_w = dr("mod_w", [4, 24, 128, 2048])
    d_ab_g = dr("ab_g", [2, 4, 128, 2048])
    d_ab_x = dr("ab_x", [2, 8, 128, 1024])
    d_ab_glu = dr("ab_glu", [2, 8, 128, 2048])
    d_ab_out = dr("ab_w_out", [2, 8, 128, 2048])
    d_c_v = dr("c_v", [2, 8, 128, 2048])
    d_c_u = dr("c_u", [2, 8, 128, 2048])
    d_c_out = dr("c_w_out", [2, 8, 128, 2048])
    d_up = dr("ffn_w_up", [4, 22, 128, 2048])
    d_down = dr("ffn_w_down", [4, 8, 128, 2816])
    d_wg = dr("lru_wg", [2, 2, 2, 8, 128, 128])
    d_wst = dr("wst", [2, 128, 8, 128])
    d_cbs = dr("c_bs", [2, 8, 128])
    d_clnb = dr("c_lnb", [2, 2048])
    d_yT = dr("yT", [1024, NT], kind="ExternalOutput")
    d_st = dr("st", [24, 1024], kind="ExternalOutput")

    P = Prog()
    cnt = [0]

    def sbuf(shape, dtype, offset):
        cnt[0] += 1
        esz = 2 if dtype == BF16 else 4
        h = nc.alloc_sbuf_tensor_at(f"b{cnt[0]}", [128] + list(shape), dtype, offset=offset)
        b = Buf(h, shape, esz, offset, "sb")
        assert offset + b.nbytes <= SB_END, (shape, offset)
        return b

    ptr = [SB_BASE]

    def alloc(shape, dtype):
        off = (ptr[0] + 31) // 32 * 32
        b = sbuf(shape, dtype, off)
        ptr[0] = off + b.nbytes
        return b

    def alias(buf, shape, dtype):
        return sbuf(shape, dtype, buf.base)

    X = alloc([8, NT], F32)
    H = alloc([8, NT], BF16)
    RSTD = alloc([NT], F32)
    SQT = alloc([8, 512], BF16)
    SQT3 = alias(SQT, [8, 2, 256], BF16)
    RING = alloc([RING_SLOTS, SLOT_BYTES // 2], BF16)
    PV = alloc([NPV], F32)
    IDF = alloc([128], F32)
    IDENT = alloc([128], BF16)
    ONES = alloc([128], BF16)
    ONE1 = alloc([128], BF16)
    MASK = alloc([5, 16], F32)
    SC = alloc([8, 2], BF16)
    SCF = alias(SC, [16], BF16)
    MODS = [alloc([96], F32), alloc([96], F32)]
    DERS = [alloc([4, 8, 2], F32), alloc([4, 8, 2], F32)]
    TMP = alloc([2, 512], F32)
    TMP3 = alias(TMP, [2, 2, 256], F32)
    SQW = alloc([3, 512], BF16)
    LC = alloc([5, 16], F32)
    BGH = alloc([32], F32)
    ST = alloc([8, 24], F32)
    STT_ = alias(SQT, [1024], F32)
    INIT = alloc([2, 6], F32)
    PHASE = (ptr[0] + 31) // 32 * 32

    ptr[0] = PHASE
    YA = alloc([8, NT], BF16)
    GP = alloc([8, 6, 286], BF16)
    DCONF = alloc([31, 128], BF16)
    TG = alloc([NT], BF16)
    TG3 = alias(TG, [6, 256], BF16)
    WGC = alloc([2, 4, 128], BF16)
    DLC = alloc([1, 4, 128], BF16)
    XBP = alloc([6, 259], BF16)
    HF = alias(RSTD, [NT], F32)
    LT = alloc([3, 3, 512], F32)
    LNT = alias(LT, [3, 512], F32)
    LNT3 = alias(LT, [3, 2, 256], F32)
    XCB2 = alias(SQT, [2, NT], BF16)
    HB = alias(TMP, [2, 512], F32)
    assert ptr[0] <= SB_END, ptr[0]

    ptr[0] = PHASE
    U = alloc([22, NT], BF16)
    ZG = alloc([6, 258], BF16)
    ZV = alloc([6, 258], BF16)
    DF = alloc([2, 3, 128], BF16)
    GT = alloc([3, 512], BF16)
    OT = alloc([3, 512], F32)
    OT3 = alias(OT, [3, 2, 256], F32)
    assert ptr[0] <= SB_END

    ptr[0] = PHASE
    V = alloc([16, 12, 128], BF16)
    US = alias(V, [16, NT], BF16)
    WST = alloc([8, 128], BF16)
    BT = alloc([16, 1, 128], F32)
    LB = alloc([16, 128], F32)
    RB = alloc([8, 128], F32)
    S1T = alloc([3, 512], F32)
    S1T4 = alias(S1T, [3, 4, 128], F32)
    UG = alloc([3, 512], BF16)
    CS1 = alloc([12, 8], F32)
    CS2 = alloc([12, 8], F32)
    CST = alloc([6, 12], F32)
    JUNK = alloc([2, 128], BF16)
    JUNKF = alloc([2, 256], F32)
    JUNKF3 = alias(JUNKF, [2, 2, 128], F32)
    assert ptr[0] <= SB_END

    ptr[0] = PHASE
    ROWV = alloc([16], F32)
    COLV = alloc([64], F32)
    PIDX = alloc([2], F32)
    FREQ = alloc([2], F32)
    PANG = alloc([64], F32)
    PKF = alloc([64], F32)
    PKI = alloc([64], I32)
    PSNR = alloc([16, 1], F32)
    PSNC = alloc([1, 64], F32)
    X3 = alias(X, [8, 24, 64], F32)

    ps_h = nc.alloc_psum_tensor("ps", [128, 8, 512], F32)
    PS = Buf(ps_h, [8, 512], 4, 0, "ps")
    PS3 = Buf(ps_h.reshape([128, 8, 2, 256]), [8, 2, 256], 4, 0, "ps")
    PS4 = Buf(ps_h.reshape([128, 8, 4, 128]), [8, 4, 128], 4, 0, "ps")
    bank_rr = [0]

    bank_pool = [4]

    def bank():
        b = bank_rr[0] % bank_pool[0]
        bank_rr[0] += 1
        return b

    sems = {}
    sem_names = ["pe", "act", "dve", "poolc", "ld", "x", "wg", "c", "out"] + [f"r{i}" for i in range(RING_SLOTS)]
    import contextlib
    stack = contextlib.ExitStack()
    for n in sem_names:
        sems[n] = stack.enter_context(nc.semaphore(n))

    def col(buf, i):
        return buf.v(slice(i, i + 1))

    def pvc(name, *idx):
        i, rem = pv_index(name, *idx)
        assert rem == 1
        return col(PV, i)

    def pvr(name, *idx):
        i, rem = pv_index(name, *idx)
        return PV.v(slice(i, i + rem))

    def isv(x):
        return isinstance(x, View)

    ACT_TBL = {AF.Sin: "T", AF.Tanh: "A", AF.Exp: "A", AF.Sqrt: "S", AF.Gelu_apprx_tanh: "G", AF.Silu: "L", AF.Ln: "N"}

    def dve_cost(v):
        return 0.08 + v.n / 960.0

    def ACT(out, in_, func, bias=0.0, scale=1.0, accum=None):
        reads = [in_] + [x for x in (bias, scale) if isv(x)]
        kw = dict(bias=bias.ap if isv(bias) else float(bias), scale=scale.ap if isv(scale) else float(scale))
        if accum is not None:
            kw["accum_out"] = accum.ap
        tbl = ACT_TBL.get(func)
        P.op("act", reads, [out] + ([accum] if accum is not None else []),
             lambda e: e.activation(out=out.ap, in_=in_.ap, func=func, **kw),
             dur=0.2 + out.n / 1200.0 + (0.1 if accum is not None else 0.0), tbl=tbl)

    def TT(out, a, b, op):
        P.op("dve", [a, b], [out], lambda e: e.tensor_tensor(out=out.ap, in0=a.ap, in1=b.ap, op=op), dur=dve_cost(out))

    def TS(out, a, s1, s2, op0, op1=None):
        reads = [a] + [x for x in (s1, s2) if isv(x)]
        a1 = s1.ap if isv(s1) else float(s1)
        a2 = (s2.ap if isv(s2) else (None if s2 is None else float(s2)))
        if op1 is None:
            P.op("dve", reads, [out], lambda e: e.tensor_scalar(out=out.ap, in0=a.ap, scalar1=a1, scalar2=None, op0=op0), dur=dve_cost(out))
        else:
            P.op("dve", reads, [out], lambda e: e.tensor_scalar(out=out.ap, in0=a.ap, scalar1=a1, scalar2=a2, op0=op0, op1=op1), dur=dve_cost(out))

    def STT(out, a, s, b, op0, op1):
        reads = [a, b] + ([s] if isv(s) else [])
        sv = s.ap if isv(s) else float(s)
        P.op("dve", reads, [out], lambda e: e.scalar_tensor_tensor(out=out.ap, in0=a.ap, scalar=sv, in1=b.ap, op0=op0, op1=op1), dur=dve_cost(out))

    def SCAN(out, a, b, init):
        reads = [a, b] + ([init] if isv(init) else [])
        iv = init.ap if isv(init) else float(init)
        P.op("dve", reads, [out], lambda e: e.tensor_tensor_scan(out=out.ap, data0=a.ap, data1=b.ap, initial=iv, op0=ALU.mult, op1=ALU.add), dur=0.08 + 2 * out.n / 960.0)

    def COPY(out, in_):
        P.op("dve", [in_], [out], lambda e: e.tensor_copy(out=out.ap, in_=in_.ap), dur=dve_cost(out))

    def RECIP(out, in_):
        P.op("dve", [in_], [out], lambda e: e.reciprocal(out=out.ap, in_=in_.ap), dur=dve_cost(out))

    def MEMSET(out, val):
        P.op("dve", [], [out], lambda e: e.memset(out.ap, float(val)), dur=dve_cost(out))

    def MM(out, pairs, start=True, stop=True):
        reads = [v for pr in pairs for v in pr]

        def emit(e):
            n = len(pairs)
            bi = None
            for i, (l, r) in enumerate(pairs):
                bi = e.matmul(out.ap, l.ap, r.ap, start=(start and i == 0), stop=(stop and i == n - 1))
            return bi
        P.op("pe", reads, [out], emit, dur=0.03 + sum(max(64, r.n) for _, r in pairs) / 2400.0)

    def MMI(outs_pairs, whole):
        reads = [v for (_, l, r) in outs_pairs for v in (l, r)]

        def emit(e):
            bi = None
            for o, l, r in outs_pairs:
                bi = e.matmul(o.ap, l.ap, r.ap, start=True, stop=True)
            return bi
        P.op("pe", reads, [whole], emit, dur=0.03 + sum(max(64, r.n) for _, _l, r in outs_pairs) / 2400.0)

    def ts(tt):
        return slice(tt * 512, (tt + 1) * 512)

    def sg(tt):
        return slice(2 * tt, 2 * tt + 2)

    ring_i = [0]
    slot_bufs = {}

    def wtile(src, kc, fw):
        slot = ring_i[0] % RING_SLOTS
        ring_i[0] += 1
        tot = kc * fw
        assert tot * 2 <= SLOT_BYTES
        key = (slot, kc, fw)
        if key not in slot_bufs:
            slot_bufs[key] = (sbuf([kc, fw], BF16, RING.base + slot * SLOT_BYTES), sbuf([tot], BF16, RING.base + slot * SLOT_BYTES))
        sb, flat = slot_bufs[key]
        if tot <= 2048:
            pairs = [(flat.v().ap, src)]
        else:
            assert tot % 2 == 0
            pairs = [(flat.v().ap.rearrange("p (a b) -> p a b", a=2), src.rearrange("p (a b) -> p a b", a=2))]
        P.dma("pool", sems[f"r{slot}"], pairs, [], [RING.v(slot)])
        return sb

    def sp_load(out_view, in_ap, sem="ld"):
        P.dma("sp", sems[sem], [(out_view.ap, in_ap)], [], [out_view])

    sp_load(PV.v(), d_pv)
    sp_load(MASK.v(), d_mask.rearrange("p (a b) -> p a b", b=16))
    sp_load(IDF.v(), d_ident)
    for kc in range(8):
        sp_load(X.v(kc), d_xT[kc * 128:(kc + 1) * 128, :], sem="x")
    COPY(IDENT.v(), IDF.v())
    MEMSET(ONES.v(), 1.0 / 1024.0)
    MEMSET(ONE1.v(), 1.0)
    MEMSET(ST.v(), 0.0)
    MEMSET(INIT.v(), 0.0)
    P.op("pool", [], [ROWV.v()], lambda e: e.iota(ROWV.v().ap, [[1, 16]], base=0, channel_multiplier=0, allow_small_or_imprecise_dtypes=True))
    P.op("pool", [], [COLV.v()], lambda e: e.iota(COLV.v().ap, [[1, 64]], base=0, channel_multiplier=0, allow_small_or_imprecise_dtypes=True))
    P.op("pool", [], [PIDX.v()], lambda e: e.iota(PIDX.v().ap, [[128, 2]], base=0, channel_multiplier=1, allow_small_or_imprecise_dtypes=True))
    ACT(FREQ.v(), PIDX.v(), AF.Exp, scale=-math.log(10000.0) / 256.0)
    TWO_PI = 2.0 * math.pi
    for kc in range(8):
        is_row = kc < 4
        n = 16 if is_row else 64
        src = ROWV.v() if is_row else COLV.v()
        phase = 0.0 if (kc % 4) < 2 else math.pi / 2.0
        ang, kf, ki = PANG.v(slice(0, n)), PKF.v(slice(0, n)), PKI.v(slice(0, n))
        TS(ang, src, col(FREQ, kc % 2), phase, ALU.mult, ALU.add)
        TS(kf, ang, 1.0 / TWO_PI, None, ALU.mult)
        COPY(ki, kf)
        COPY(kf, ki)
        STT(ang, kf, -TWO_PI, ang, ALU.mult, ALU.add)
        if is_row:
            flat = alias(PSNR, [16], F32).v()
            bview = PSNR.v()
        else:
            flat = alias(PSNC, [64], F32).v()
            bview = PSNC.v()
        ACT(flat, ang, AF.Sin)
        TS(flat, flat, pvc("pm", 0), None, ALU.mult)
        bb = View(bview.ap.broadcast_to([128, 16, 64]), bview.sp, bview.lo, bview.hi, 1024)
        TT(X3.v(kc, slice(0, 16)), X3.v(kc, slice(0, 16)), bb, ALU.add)
    ACT(SCF.v(), pvr("cond"), AF.Silu)

    def mod_steps(l, mb, dst):
        steps = []
        for t in range(24):
            def step(t=t):
                sl = wtile(d_mod_w[l, t], 8, 256)
                for f in range(2):
                    fc = 2 * t + f
                    MM(PS.v(mb, slice(2 * fc, 2 * fc + 2)),
                       [(sl.v(kc, slice(128 * f, 128 * f + 128)), SC.v(kc)) for kc in range(8)])
                if t % 4 == 3:
                    part = t // 4
                    off, _ = pv_index("mod_b", l)
                    for ci in range(2):
                        sel = slice(16 * part + ci, 16 * (part + 1), 2)
                        TT(dst.v(sel), PS.v(mb, sel), PV.v(slice(off + 8 * part, off + 8 * part + 8)), ALU.add)
            steps.append(step)
        return steps

    def derive(l, mod, DER, which=(0, 1, 2, 3)):
        for ci in range(2):
            def m(part):
                return mod.v(slice(2 * 8 * part + ci, 2 * 8 * (part + 1), 2))
            if 0 in which:
                STT(DER.v(0, None, ci), m(1), 1.0, pvr("norm_g", l, 0), ALU.add, ALU.mult)
            if 1 in which:
                TT(DER.v(1, None, ci), m(2), pvr("norm_g", l, 1), ALU.mult)
            if 2 in which:
                STT(DER.v(2, None, ci), m(4), 1.0, pvr("norm_g", l, 2), ALU.add, ALU.mult)
            if 3 in which:
                TT(DER.v(3, None, ci), m(5), pvr("norm_g", l, 3), ALU.mult)

    def mcol(mod, part, kc, ci):
        i = 2 * (8 * part + kc) + ci
        return col(mod, i)

    def norm_mod(der, which_a, mod, part_b, between=None):
        for tt in range(3):
            ci = 0 if tt < 2 else 1
            for kc in range(8):
                ACT(SQT.v(kc), X.v(kc, ts(tt)), AF.Square)
            MM(PS.v(5 + tt), [(ONES.v(), SQT.v(kc)) for kc in range(8)])
            ACT(RSTD.v(ts(tt)), PS.v(5 + tt), AF.Sqrt, bias=EPS)
            RECIP(RSTD.v(ts(tt)), RSTD.v(ts(tt)))
            for kc in range(8):
                t = TMP.v(kc % 2)
                TT(t, X.v(kc, ts(tt)), RSTD.v(ts(tt)), ALU.mult)
                ACT(H.v(kc, ts(tt)), t, AF.Identity, scale=der.v(which_a, slice(kc, kc + 1), ci), bias=mcol(mod, part_b, kc, ci))
            if between is not None:
                between()

    GSIDE = []

    def run_gside(n):
        for _ in range(n):
            if GSIDE:
                GSIDE.pop(0)()

    def drain_gside():
        run_gside(len(GSIDE))

    def out_proj(w2d, nk, rhs_fn, three_d, der_g, which_g, nxt, pre=None, fill=None, d2=0):
        cnt_ = [0]
        drain_gside()
        side = GSIDE

        def run_side(n):
            for _ in range(n):
                if side:
                    side.pop(0)()

        def groups(tts, nside):
            P.tag = f'outproj.g{tts}'
            pend = []

            def stat(item):
                sq, fc, tt = item
                MM(PS.v(5 + tt), [(ONES.v(), sq)], start=(fc == 0), stop=(fc == 7))
            for fc in range(8):
                sl = wtile(w2d[fc], nk, 128)
                for tt in tts:
                    b = bank()
                    o = PS3.v(b) if three_d else PS.v(b)
                    MM(o, [(sl.v(kc), rhs_fn(kc, tt)) for kc in range(nk)])
                    ACT(H.v(fc, ts(tt)), PS.v(b), AF.Copy)
                    sq = SQW.v(cnt_[0] % 3)
                    cnt_[0] += 1
                    ACT(sq, PS.v(b), AF.Square)
                    pend.append((sq, fc, tt))
                    if len(pend) > 2:
                        stat(pend.pop(0))
                    run_side(nside)
            while pend:
                stat(pend.pop(0))

        def tail_steps(tt):
            ci = 0 if tt < 2 else 1

            def s0():
                P.tag = f'tail{tt}'
                ACT(RSTD.v(ts(tt)), PS.v(5 + tt), AF.Sqrt, bias=EPS)
                RECIP(RSTD.v(ts(tt)), RSTD.v(ts(tt)))

            def sk(kc):
                P.tag = f'tail{tt}'
                if kc < 8:
                    t = TMP.v(kc % 2)
                    STT(t, H.v(kc, ts(tt)), der_g.v(which_g, slice(kc, kc + 1), ci), RSTD.v(ts(tt)), ALU.mult, ALU.mult)
                    TT(X.v(kc, ts(tt)), X.v(kc, ts(tt)), t, ALU.add)
                if nxt is not None and kc >= 1:
                    ACT(SQT.v(kc - 1), X.v(kc - 1, ts(tt)), AF.Square)
            return [s0] + [lambda kc=kc: sk(kc) for kc in range(9 if nxt is not None else 8)]

        def part2_steps(tt):
            if nxt is None:
                return []
            der_n, which_a, mod_n, part_b = nxt
            ci = 0 if tt < 2 else 1

            def s0():
                P.tag = f'part2_{tt}'
                MM(PS.v(5 + tt), [(ONES.v(), SQT.v(kc)) for kc in range(8)])
                ACT(RSTD.v(ts(tt)), PS.v(5 + tt), AF.Sqrt, bias=EPS)
                RECIP(RSTD.v(ts(tt)), RSTD.v(ts(tt)))

            def sk(kc):
                P.tag = f'part2_{tt}'
                if kc < 8:
                    TT(TMP.v(kc % 2), X.v(kc, ts(tt)), RSTD.v(ts(tt)), ALU.mult)
                if kc >= 1:
                    k1 = kc - 1
                    ACT(H.v(k1, ts(tt)), TMP.v(k1 % 2), AF.Identity, scale=der_n.v(which_a, slice(k1, k1 + 1), ci), bias=mcol(mod_n, part_b, k1, ci))
            return [s0] + [lambda kc=kc: sk(kc) for kc in range(9)]

        if pre is not None:
            for st_ in pre(0) + pre(1):
                st_()
            side.extend(pre(2))
        for tt in range(3):
            groups([tt], NSIDE)
            side.extend(tail_steps(tt))
            if nxt is not None:
                side.extend([(lambda: None)] * (PART2_DELAY if tt < 2 else d2))
            side.extend(part2_steps(tt))
        if fill is not None:
            fill()
        if nxt is None:
            run_side(len(side))

    def interleave(*gens):
        gens = list(gens)
        while gens:
            for g in list(gens):
                try:
                    next(g)
                except StopIteration:
                    gens.remove(g)

    def POOL_TS(out, a, s1, op0):
        P.op("pool", [a, s1], [out], lambda e: e.tensor_scalar(out=out.ap, in0=a.ap, scalar1=s1.ap, scalar2=None, op0=op0))

    def mixer_ab(jj, der, nxt, fill):
        YA3 = ya3[0]
        sls = {}

        def gate_group(j, f, tt):
            fc = 2 * j + f
            b = bank()
            MM(PS.v(b), [(sls[j].v(kc, slice(128 * f, 128 * f + 128)), H.v(kc, ts(tt))) for kc in range(8)])
            ACT(YA.v(fc, ts(tt)), PS.v(b), AF.Gelu_apprx_tanh)
        for j in range(3):
            P.tag = f'gate_br{j}'
            sls[j] = wtile(d_ab_g[jj, j], 8, 256)
            for f in range(2):
                for tt in range(2):
                    gate_group(j, f, tt)
                    run_gside(2)
        drain_gside()
        for j in range(3):
            for f in range(2):
                gate_group(j, f, 2)
        P.tag = 'gate_br3'
        sls[3] = wtile(d_ab_g[jj, 3], 8, 256)
        for f in range(2):
            for tt in range(3):
                gate_group(3, f, tt)
        lam = pvr("lru_lam", jj)
        ACT(LC.v(0), lam, AF.Exp, scale=-1.0)
        ACT(LC.v(0), LC.v(0), AF.Ln, bias=1.0)
        TS(LC.v(1), LC.v(0), -4.0, None, ALU.mult)
        TS(BGH.v(), pvr("lru_bg", jj), 0.5, None, ALU.mult)
        TS(pvr("cf_cw", jj), pvr("cf_cw", jj), 0.5, None, ALU.mult)
        MEMSET(XBP.v(0, slice(0, 2)), 0.0)
        MEMSET(XBP.v(5, slice(258, 259)), 0.0)
        MEMSET(GP.v(None, 0, slice(0, 15)), 0.0)
        MEMSET(GP.v(None, 5, slice(271, 286)), 0.0)
        bank_pool[0] = 8

        def lru_pre(c):
            P.tag = f'lru_pre{c}'
            sl = wtile(d_ab_x[jj, c], 8, 128)
            wq = c % 2
            P.dma("pool", sems["wg"], [(WGC.v(wq).ap, d_wg[jj][:, :, c].rearrange("d g i j -> i (d g) j"))], [], [WGC.v(wq)])
            for k in range(4):
                TS(DLC.v(0, k), IDENT.v(), pvc("lru_cw", jj, k, c), None, ALU.mult)
            for tt in range(3):
                b = bank()
                MM(PS.v(b), [(sl.v(kc), H.v(kc, ts(tt))) for kc in range(8)])
                ACT(XBP.v(sg(tt), slice(2, 258)), PS3.v(b), AF.Copy)
            TT(XBP.v(slice(1, 6), slice(0, 2)), XBP.v(slice(0, 5), slice(256, 258)), MASK.v(None, slice(0, 2)), ALU.mult)
            TT(XBP.v(slice(0, 5), slice(258, 259)), XBP.v(slice(1, 6), slice(2, 3)), MASK.v(None, slice(0, 1)), ALU.mult)
            for tt in range(3):
                b = bank()
                MM(PS3.v(b), [(DLC.v(0, k), XBP.v(sg(tt), slice(k, k + 256))) for k in range(4)])
                ACT(XCB2.v(wq, ts(tt)), PS.v(b), AF.Identity, bias=pvc("lru_cb", jj, c))

        def lru_chunk(c):
            wq = c % 2
            if c == 0:
                lru_pre(0)
                yield
            for d in range(2):
                order = [0, 1, 2] if d == 0 else [2, 1, 0]
                chc = LC.v(1, slice(d * 8 + c, d * 8 + c + 1))
                if d == 1 and c < 7:
                    lru_pre(c + 1)
                P.tag = f'lru{c}.d{d}s1'
                for q, tt in enumerate(order):
                    Aa, TI, S = (LT.v(q, z) for z in range(3))
                    br = bank()
                    MM(PS.v(br), [(WGC.v(wq, d * 2 + 0), XCB2.v(wq, ts(tt)))])
                    bi_ = bank()
                    MM(PS.v(bi_), [(WGC.v(wq, d * 2 + 1), XCB2.v(wq, ts(tt)))])
                    ACT(Aa, PS.v(br), AF.Tanh, scale=0.5, bias=col(BGH, (d * 2 + 0) * 8 + c))
                    ACT(TI, PS.v(bi_), AF.Tanh, scale=0.5, bias=col(BGH, (d * 2 + 1) * 8 + c))
                    ACT(Aa, Aa, AF.Exp, scale=chc, bias=chc)
                    STT(S, Aa, -1.0, Aa, ALU.mult, ALU.mult)
                    TS(S, S, 1.0, 0.0, ALU.add, ALU.max)
                    STT(TI, TI, 1.0, XCB2.v(wq, ts(tt)), ALU.add, ALU.mult)
                    if d == 0:
                        TT(LT.v(q, 0, slice(256, 257)), LT.v(q, 0, slice(256, 257)), MASK.v(2 * tt, slice(0, 1)), ALU.mult)
                        if tt == 1:
                            TT(LT.v(q, 0, slice(0, 1)), LT.v(q, 0, slice(0, 1)), MASK.v(1, slice(0, 1)), ALU.mult)
                    else:
                        TT(LT.v(q, 0, slice(255, 256)), LT.v(q, 0, slice(255, 256)), MASK.v(2 * tt, slice(0, 1)), ALU.mult)
                        if tt == 0:
                            TT(LT.v(q, 0, slice(511, 512)), LT.v(q, 0, slice(511, 512)), MASK.v(1, slice(0, 1)), ALU.mult)
                yield
                P.tag = f'lru{c}.d{d}s2'
                for q, tt in enumerate(order):
                    Aa, TI, S = (LT.v(q, z) for z in range(3))
                    ACT(S, S, AF.Sqrt, scale=0.25)
                for q, tt in enumerate(order):
                    Aa, TI, S = (LT.v(q, z) for z in range(3))
                    TT(TI, TI, S, ALU.mult)
                    r0 = (jj * 2 + d) * 6
                    if d == 0:
                        init = pvc("h0", jj, 0, c) if tt == 0 else (HF.v(slice(511, 512)) if tt == 1 else 0.0)
                        SCAN(HF.v(ts(tt)), Aa, TI, init)
                        if tt == 2:
                            COPY(ST.v(c, slice(r0, r0 + 6)), HF.v(slice(255, NT, 256)))
                    else:
                        hb = HB.v(q % 2)
                        init = 0.0 if tt == 2 else (pvc("h0", jj, 1, c) if tt == 1 else ST.v(c, slice(r0 + 2, r0 + 3)))
                        SCAN(rev(hb), rev(Aa), rev(TI), init)
                        COPY(ST.v(c, slice(r0 + 2 * tt, r0 + 2 * tt + 2)), HB.v(q % 2, slice(0, 512, 256)))
                        TT(hb, hb, HF.v(ts(tt)), ALU.add)
                        TT(YA.v(c, ts(tt)), YA.v(c, ts(tt)), hb, ALU.mult)
                yield

        def dconf_builds(c):
            for k in range(31):
                if k % 2 == 0:
                    ACT(DCONF.v(k), IDENT.v(), AF.Identity, scale=pvc("cf_cw", jj, k, c))
                else:
                    TS(DCONF.v(k), IDENT.v(), pvc("cf_cw", jj, k, c), None, ALU.mult)

        def conf_chunk(c):
            P.tag = f'conf{c}.glub'
            sl = wtile(d_ab_glu[jj, c], 8, 256)
            if c == 0:
                dconf_builds(0)
            for tt in range(3):
                b = bank()
                MM(PS.v(b), [(sl.v(kc, slice(128, 256)), H.v(kc, ts(tt))) for kc in range(8)])
                ACT(TG.v(ts(tt)), PS.v(b), AF.Tanh, scale=0.5)
            P.tag = f'conf{c}.glua'
            for tt in range(3):
                b = bank()
                MM(PS.v(b), [(sl.v(kc, slice(0, 128)), H.v(kc, ts(tt))) for kc in range(8)])
                STT(GP.v(c, sg(tt), slice(15, 271)), TG3.v(sg(tt)), 1.0, PS3.v(b), ALU.add, ALU.mult)
            TT(GP.v(c, slice(1, 6), slice(0, 15)), GP.v(c, slice(0, 5), slice(256, 271)), MASK.v(None, slice(0, 15)), ALU.mult)
            TT(GP.v(c, slice(0, 5), slice(271, 286)), GP.v(c, slice(1, 6), slice(15, 30)), MASK.v(None, slice(0, 15)), ALU.mult)
            yield
            P.tag = f'conf{c}.conv'
            for tt in range(3):
                b = bank()
                MM(PS3.v(b), [(DCONF.v(k), GP.v(c, sg(tt), slice(k, k + 256))) for k in range(31)])
                ACT(GP.v(c, sg(tt), slice(15, 271)), PS3.v(b), AF.Identity, bias=pvc("cf_cb", jj, c))
                if tt == 2 and c < 7:
                    P.tag = f'conf{c}.builds'
                    dconf_builds(c + 1)
                yield

        for c in range(8):
            interleave(lru_chunk(c), conf_chunk(c))
        bank_pool[0] = 4

        def ln_tile(tt):
            def s0():
                for c in range(8):
                    ACT(SQT3.v(c), GP.v(c, sg(tt), slice(15, 271)), AF.Square)
                bm = bank()
                MM(PS3.v(bm), [(ONES.v(), GP.v(c, sg(tt), slice(15, 271))) for c in range(8)])
                be = bank()
                MM(PS.v(be), [(ONES.v(), SQT.v(c)) for c in range(8)])
                ACT(LNT.v(0), PS.v(bm), AF.Copy)
                TT(LNT.v(1), LNT.v(0), LNT.v(0), ALU.mult)
                TT(LNT.v(1), PS.v(be), LNT.v(1), ALU.subtract)
                TS(LNT.v(1), LNT.v(1), 0.0, None, ALU.max)
                ACT(LNT.v(1), LNT.v(1), AF.Sqrt, bias=EPS)
                RECIP(LNT.v(1), LNT.v(1))
                TT(LNT.v(2), LNT.v(0), LNT.v(1), ALU.mult)

            def sk(c):
                t = TMP3.v(c % 2)
                g = GP.v(c, sg(tt), slice(15, 271))
                TT(t, g, LNT3.v(1), ALU.mult)
                TT(t, t, LNT3.v(2), ALU.subtract)
                ACT(g, t, AF.Silu, scale=pvc("cf_g", jj, c), bias=pvc("cf_b", jj, c))
            return [s0] + [lambda c=c: sk(c) for c in range(8)]

        def rhs3(kc, tt):
            return YA3.v(kc, sg(tt)) if kc < 8 else GP.v(kc - 8, sg(tt), slice(15, 271))
        out_proj(d_ab_out[jj], 16, rhs3, True, der, 1, nxt, pre=ln_tile, fill=fill, d2=D2_FFN)

    ya3 = [alias(YA, [8, 6, 256], BF16)]

    def mixer_c(jj, der, nxt, fill):
        P.dma("pool", sems["wg"], [(WST.v().ap, d_wst[jj])], [], [WST.v()])
        MEMSET(LB.v(parts=slice(0, 2)), 1.0)
        P.dma("sp", sems["c"], [(LB.v(parts=slice(0, 1)).ap, d_clnb[jj:jj + 1, :].rearrange("o (a b) -> o a b", b=128))], [], [LB.v()])
        P.dma("sp", sems["c"], [(RB.v(parts=slice(1, 2)).ap, d_cbs[jj:jj + 1, :, :])], [], [RB.v()])
        vsl = {}

        def v_group(sI, m):
            sl = vsl[sI]
            b = bank()
            MM(PS.v(b, slice(0, 256)), [(H.v(kc, slice(128 * m, 128 * m + 128)), sl.v(kc)) for kc in range(8)])
            vv = V.v(slice(2 * sI, 2 * sI + 2), m)
            ACT(vv, PS4.v(b, slice(0, 2)), AF.Gelu_apprx_tanh, accum=CS1.v(m, slice(sI, sI + 1)))
            jf = JUNKF.v((sI * 12 + m) % 2)
            TT(JUNKF3.v((sI * 12 + m) % 2), vv, vv, ALU.mult)
            P.op("dve", [jf], [CS2.v(m, slice(sI, sI + 1))], lambda e, jf=jf, o=CS2.v(m, slice(sI, sI + 1)): e.reduce_sum(out=o.ap, in_=jf.ap, axis=AX.X), dur=0.35)
        for sI in range(3):
            P.tag = f'c_v{sI}'
            vsl[sI] = wtile(d_c_v[jj, sI], 8, 256)
            for m in range(8):
                v_group(sI, m)
                run_gside(1)
        drain_gside()
        for sI in range(3):
            for m in range(8, 12):
                v_group(sI, m)
        for sI in range(3, 8):
            P.tag = f'c_v{sI}'
            vsl[sI] = wtile(d_c_v[jj, sI], 8, 256)
            for m in range(12):
                v_group(sI, m)
        P.op("dve", [CS1.v()], [CST.v(0)], lambda e: e.reduce_sum(out=CST.v(0).ap, in_=CS1.v().ap, axis=AX.X))
        P.op("dve", [CS2.v()], [CST.v(1)], lambda e: e.reduce_sum(out=CST.v(1).ap, in_=CS2.v().ap, axis=AX.X))
        TS(CST.v(2), CST.v(0), 1.0 / 2048.0, None, ALU.mult)
        TT(CST.v(3), CST.v(2), CST.v(2), ALU.mult)
        STT(CST.v(4), CST.v(1), 1.0 / 2048.0, CST.v(3), ALU.mult, ALU.subtract)
        TS(CST.v(4), CST.v(4), 0.0, None, ALU.max)
        ACT(CST.v(4), CST.v(4), AF.Sqrt, bias=EPS)
        RECIP(CST.v(4), CST.v(4))
        STT(CST.v(5), CST.v(2), -1.0, CST.v(4), ALU.mult, ALU.mult)
        for m in range(12):
            if m % 2 == 0:
                TS(V.v(None, m), V.v(None, m), CST.v(4, slice(m, m + 1)), CST.v(5, slice(m, m + 1)), ALU.mult, ALU.add)
            else:
                ACT(V.v(None, m), V.v(None, m), AF.Identity, scale=CST.v(4, slice(m, m + 1)), bias=CST.v(5, slice(m, m + 1)))
        for h in range(8):
            b = bank()
            MM(PS.v(b, slice(0, 128), parts=slice(0, 1)), [(ONE1.v(slice(0, 1)), WST.v(h))])
            ACT(RB.v(h, parts=slice(0, 1)), PS.v(b, slice(0, 128), parts=slice(0, 1)), AF.Copy)
        for dc in range(16):
            b = bank()
            MM(PS.v(b, slice(0, 128)), [(LB.v(dc, parts=slice(0, 2)), RB.v(dc // 2, parts=slice(0, 2)))])
            ACT(BT.v(dc, 0), PS.v(b, slice(0, 128)), AF.Copy)
        i = 0
        for h in range(8):
            P.tag = f'c_u{h}'
            sl = wtile(d_c_u[jj, h], 8, 256)
            for f in range(2):
                dc = 2 * h + f
                for tt in range(3):
                    b = bank()
                    MMI([(PS.v(b, slice(128 * n, 128 * n + 128)), V.v(dc, 4 * tt + n), WST.v(h)) for n in range(4)], PS.v(b))
                    s1 = S1T4.v(i % 3)
                    bt = BT.v(dc)
                    btb = View(bt.ap.broadcast_to([128, 4, 128]), bt.sp, bt.lo, bt.hi, 512)
                    STT(s1, PS4.v(b), pvc("c_lng", jj, dc), btb, ALU.mult, ALU.add)
                    b2 = bank()
                    MM(PS.v(b2), [(sl.v(kc, slice(128 * f, 128 * f + 128)), H.v(kc, ts(tt))) for kc in range(8)])
                    ug = UG.v(i % 3)
                    ACT(ug, PS.v(b2), AF.Gelu_apprx_tanh)
                    TT(US.v(dc, ts(tt)), ug, S1T.v(i % 3), ALU.mult)
                    i += 1
        out_proj(d_c_out[jj], 16, lambda kc, tt: US.v(kc, ts(tt)), False, der, 1, nxt, fill=fill, d2=D2_FFN)

    def ffn(l, extra_steps, der, nxt_fn):
        for Z in (ZG, ZV):
            MEMSET(Z.v(0, slice(0, 1)), 0.0)
            MEMSET(Z.v(5, slice(257, 258)), 0.0)
        for j in range(22):
            P.tag = f'ffn_up{j}'
            sl = wtile(d_up[l, j], 8, 256)
            ds_ = j % 2
            for k in range(3):
                TS(DF.v(ds_, k), IDENT.v(), pvc("f_cw", l, k, j), None, ALU.mult)
            def up_gate(tt):
                b = bank()
                MM(PS.v(b), [(sl.v(kc, slice(0, 128)), H.v(kc, ts(tt))) for kc in range(8)])
                ACT(ZG.v(sg(tt), slice(1, 257)), PS3.v(b), AF.Copy)

            def up_val(tt):
                b = bank()
                MM(PS.v(b), [(sl.v(kc, slice(128, 256)), H.v(kc, ts(tt))) for kc in range(8)])
                ACT(ZV.v(sg(tt), slice(1, 257)), PS3.v(b), AF.Copy)
                ACT(OT3.v(tt), PS3.v(b), AF.Identity, scale=pvc("f_cw", l, 1, 22 + j), bias=pvc("f_cb", l, 22 + j))
            if j == 0:
                for fn, tt in ((up_gate, 0), (up_gate, 1), (up_val, 0), (up_val, 1)):
                    fn(tt)
                    run_gside(5)
                drain_gside()
                up_gate(2)
                up_val(2)
            else:
                for tt in range(3):
                    up_gate(tt)
                for tt in range(3):
                    up_val(tt)
            TT(ZG.v(slice(1, 6), slice(0, 1)), ZG.v(slice(0, 5), slice(256, 257)), MASK.v(None, slice(0, 1)), ALU.mult)
            TT(ZG.v(slice(0, 5), slice(257, 258)), ZG.v(slice(1, 6), slice(1, 2)), MASK.v(None, slice(0, 1)), ALU.mult)
            TT(ZV.v(slice(1, 6), slice(0, 1)), ZV.v(slice(0, 5), slice(256, 257)), MASK.v(None, slice(0, 1)), ALU.mult)
            TT(ZV.v(slice(0, 5), slice(257, 258)), ZV.v(slice(1, 6), slice(1, 2)), MASK.v(None, slice(0, 1)), ALU.mult)
            for tt in range(3):
                b = bank()
                MM(PS3.v(b), [(DF.v(ds_, k), ZG.v(sg(tt), slice(k, k + 256))) for k in range(3)])
                ACT(GT.v(tt), PS.v(b), AF.Gelu_apprx_tanh, bias=pvc("f_cb", l, j))
            for tt in range(3):
                o = OT3.v(tt)
                STT(o, ZV.v(sg(tt), slice(0, 256)), pvc("f_cw", l, 0, 22 + j), o, ALU.mult, ALU.add)
                STT(o, ZV.v(sg(tt), slice(2, 258)), pvc("f_cw", l, 2, 22 + j), o, ALU.mult, ALU.add)
                TT(U.v(j, ts(tt)), OT.v(tt), GT.v(tt), ALU.mult)
            if extra_steps and j % 4 != 3:
                extra_steps.pop(0)()
        while extra_steps:
            extra_steps.pop(0)()
        out_proj(d_down[l], 22, lambda kc, tt: U.v(kc, ts(tt)), False, der, 3, nxt_fn(), d2=(D2_AB if (l + 1) % 2 == 0 else D2_C))

    steps0 = mod_steps(0, 4, MODS[0])
    for _ in range(8):
        steps0.pop(0)()
    derive(0, MODS[0], DERS[0], which=(0,))

    def between0():
        for _ in range(6):
            if steps0:
                steps0.pop(0)()
    norm_mod(DERS[0], 0, MODS[0], 0, between=between0)
    while steps0:
        steps0.pop(0)()
    derive(0, MODS[0], DERS[0], which=(1, 2, 3))
    for l in range(n_layers):
        mod = MODS[l % 2]
        der = DERS[l % 2]
        last = (l + 1 == n_layers)
        steps = mod_steps(l + 1, 4, MODS[(l + 1) % 2]) if not last else []

        def fill(steps=steps):
            for _ in range(8):
                if steps:
                    steps.pop(0)()
        if l % 2 == 0:
            mixer_ab(l // 2, der, (der, 2, mod, 3), fill)
        else:
            mixer_c(l // 2, der, (der, 2, mod, 3), fill)

        def nxt_fn(l=l, last=last):
            if last:
                return None
            derive(l + 1, MODS[(l + 1) % 2], DERS[(l + 1) % 2])
            return (DERS[(l + 1) % 2], 0, MODS[(l + 1) % 2], 0)
        ffn(l, steps, der, nxt_fn)

    drain_gside()
    for kc in range(8):
        P.dma("sp", sems["out"], [(d_yT[kc * 128:(kc + 1) * 128, :], X.v(kc).ap)], [X.v(kc)], [])
    for c in range(8):
        b = bank()
        o = PS.v(b, slice(0, 128), parts=slice(0, 24))
        P.op("pe", [ST.v(c), IDF.v()], [o], lambda e, o=o, c=c: e.transpose(out=o.ap, in_=ST.v(c).ap, identity=IDF.v().ap))
        ACT(STT_.v(slice(128 * c, 128 * c + 128), parts=slice(0, 24)), o, AF.Copy)
    P.dma("sp", sems["out"], [(d_st, STT_.v(parts=slice(0, 24)).ap)], [STT_.v()], [])
    fin = P.op("sp", [], [], lambda e: None)
    for ins in P.q["sp"]:
        if ins.dma_sem is sems["out"]:
            fin.deps.add(ins)

    for sname in ("ld", "x"):
        tot = 16 * P.dma_count.get(sems[sname], 0)
        for ins in P.q["sp"]:
            if ins.dma_sem is sems[sname]:
                ins.dma_val = tot

    if MODEL_REPORT and not SCHEDULE:
        print(f"[kernel] {len(P.all)} instructions; model makespan {P.simulate():.0f} us", flush=True)
    if SCHEDULE:
        base = P.simulate()
        est = P.schedule()
        chk = P.simulate()
        print(f"[kernel] {len(P.all)} instructions; model makespan recorded order {base:.0f} us -> list-scheduled {est:.0f} us (in-order replay {chk:.0f} us)", flush=True)

    eng_sem = {"pe": sems["pe"], "act": sems["act"], "dve": sems["dve"], "pool": sems["poolc"]}
    for e in ("pe", "act", "dve", "pool"):
        n = 0
        for ins in P.q[e]:
            if ins.flag and ins.dma_sem is None:
                n += 1
                ins.ms = n

    def run_queue(ename, eng):
        known = {}
        for ins in P.q[ename]:
            need = {}
            for d in ins.deps:
                if d.dma_sem is not None:
                    s, v = d.dma_sem, d.dma_val
                else:
                    s, v = eng_sem[d.eng], d.ms
                    assert v > 0
                if need.get(id(s), (None, 0))[1] < v:
                    need[id(s)] = (s, v)
            for sid, (s, v) in need.items():
                if known.get(sid, 0) >= v:
                    continue
                eng.wait_ge(s, v)
                known[sid] = v
            bi = ins.emit(eng)
            if ins.flag and ins.dma_sem is None:
                assert bi is not None
                bi.then_inc(eng_sem[ename], 1)

    with stack:
        with nc.Block() as block:
            @block.tensor
            def _(e):
                run_queue("pe", e)

            @block.scalar
            def _(e):
                run_queue("act", e)

            @block.vector
            def _(e):
                run_queue("dve", e)

            @block.gpsimd
            def _(e):
                run_queue("pool", e)

            @block.sync
            def _(e):
                run_queue("sp", e)
    return nc


def ffn_mod_bank_guard(steps):
    return steps


def to_fm(v):
    v = np.asarray(v, np.float32)
    lead = v.shape[:-1]
    n = v.shape[-1] // 128
    return np.ascontiguousarray(np.moveaxis(v.reshape(*lead, n, 128), -1, 0))


_NC_CACHE = {}


def kernel(x_prompt, x_sample, state_lru, c, c_ctx, mod_w, mod_b, norm_g,
           ab_w_in, lru_conv_w, lru_conv_b, lru_w_gates, lru_b_gates, lru_lambda,
           conf_conv_w, conf_conv_b, conf_ln_g, conf_ln_b, ab_w_out,
           c_w_in, c_ln_g, c_ln_b, c_w_s, c_b_s, c_w_out,
           ffn_w_up, ffn_conv_w, ffn_conv_b, ffn_w_down, _n_layers=4):
    f = lambda a: np.ascontiguousarray(np.asarray(a, np.float32))
    x_prompt, x_sample, state_lru, c, c_ctx = map(f, (x_prompt, x_sample, state_lru, c, c_ctx))
    shared = {
        "norm_g": to_fm(norm_g), "mod_b": to_fm(mod_b), "lru_cw": to_fm(lru_conv_w), "lru_cb": to_fm(lru_conv_b),
        "lru_bg": to_fm(lru_b_gates), "lru_lam": to_fm(lru_lambda), "cf_cw": to_fm(conf_conv_w),
        "cf_cb": to_fm(conf_conv_b), "cf_g": to_fm(conf_ln_g), "cf_b": to_fm(conf_ln_b),
        "c_lng": to_fm(c_ln_g), "f_cw": to_fm(ffn_conv_w), "f_cb": to_fm(ffn_conv_b),
    }
    ident = np.eye(128, dtype=np.float32)
    wst = np.ascontiguousarray(np.transpose(f(c_w_s), (0, 3, 1, 2)))
    def tile_cols(w, col_groups):
        L, K, _ = w.shape
        kc = K // 128
        tiles = []
        for groups_ in col_groups:
            blk = np.concatenate([w[:, :, a:b] for a, b in groups_], axis=2)
            fw = blk.shape[2]
            tiles.append(blk.reshape(L, kc, 128, fw).transpose(0, 2, 1, 3).reshape(L, 128, kc * fw))
        return np.ascontiguousarray(np.stack(tiles, axis=1))

    ab_in, c_in, up_w = f(ab_w_in), f(c_w_in), f(ffn_w_up)
    common = {
        "ident": ident,
        "mod_w": tile_cols(f(mod_w), [[(256 * t, 256 * t + 256)] for t in range(24)]),
        "ab_g": tile_cols(ab_in, [[(256 * j, 256 * j + 256)] for j in range(4)]),
        "ab_x": tile_cols(ab_in, [[(1024 + 128 * c_, 1024 + 128 * c_ + 128)] for c_ in range(8)]),
        "ab_glu": tile_cols(ab_in, [[(2048 + 128 * c_, 2048 + 128 * c_ + 128), (3072 + 128 * c_, 3072 + 128 * c_ + 128)] for c_ in range(8)]),
        "ab_w_out": tile_cols(f(ab_w_out), [[(128 * q, 128 * q + 128)] for q in range(8)]),
        "c_v": tile_cols(c_in, [[(2048 + 256 * q, 2048 + 256 * q + 256)] for q in range(8)]),
        "c_u": tile_cols(c_in, [[(256 * q, 256 * q + 256)] for q in range(8)]),
        "c_w_out": tile_cols(f(c_w_out), [[(128 * q, 128 * q + 128)] for q in range(8)]),
        "ffn_w_up": tile_cols(up_w, [[(128 * q, 128 * q + 128), (2816 + 128 * q, 2816 + 128 * q + 128)] for q in range(22)]),
        "ffn_w_down": tile_cols(f(ffn_w_down), [[(128 * q, 128 * q + 128)] for q in range(8)]),
        "lru_wg": f(lru_w_gates), "wst": wst, "c_bs": f(c_b_s), "c_lnb": f(c_ln_b),
    }
    prompts_of = []
    in_maps = []
    for core in range(8):
        if core < 4:
            pr = [2 * core, 2 * core + 1]
            xs = np.concatenate([x_sample[core]] + [x_prompt[p] for p in pr], axis=0)
            condS = c[core]
            h0 = state_lru[core]
            m = [1.0, 1.0, 1.0, 0.0, 0.0]
            pm = 1.0
        else:
            pr = list(range(8 + 6 * (core - 4), 8 + 6 * (core - 4) + 6))
            xs = np.concatenate([x_prompt[p] for p in pr], axis=0)
            condS = c_ctx
            h0 = np.zeros((2, 2, 1024), np.float32)
            m = [0.0] * 5
            pm = 0.0
        prompts_of.append(pr)
        pvd = dict(shared)
        pvd["h0"] = to_fm(h0)
        pvd["cond"] = np.ascontiguousarray(np.moveaxis(to_fm(np.stack([condS, c_ctx], 0)), 1, 2))
        pvd["pm"] = np.full((128, 1), pm, np.float32)
        pv = np.concatenate([pvd[n].reshape(128, -1) for n, _ in PV_SPEC], axis=1).astype(np.float32)
        assert pv.shape == (128, NPV), pv.shape
        mask = np.ascontiguousarray(np.broadcast_to(np.asarray(m, np.float32)[None, :, None], (128, 5, 16)).reshape(128, 80))
        d = dict(common)
        d.update({"xT": np.ascontiguousarray(xs.T), "pv": pv, "mask": mask})
        in_maps.append(d)
    if _n_layers not in _NC_CACHE:
        _NC_CACHE[_n_layers] = build(_n_layers)
    nc = _NC_CACHE[_n_layers]
    res = run_bass_kernel_spmd(nc, in_maps, core_ids=list(range(8)))
    y_prompt = np.zeros((32, 256, 1024), np.float32)
    y_sample = np.zeros((4, 1024, 1024), np.float32)
    new_state = np.zeros((32, 2, 2, 1024), np.float32)
    for core in range(8):
        y = np.asarray(res.results[core]["yT"], np.float32).T
        st = np.asarray(res.results[core]["st"], np.float32).reshape(2, 2, 6, 1024)
        if core < 4:
            y_sample[core] = y[:1024]
            segs = [4, 5]
        else:
            segs = list(range(6))
        for s, p in zip(segs, prompts_of[core]):
            y_prompt[p] = y[s * 256:(s + 1) * 256]
            new_state[p] = st[:, :, s, :]
    return (y_prompt, y_sample, new_state)
```

```python
import math
import numpy as np
import concourse.bass as bass
import concourse.mybir as mybir
from concourse.bass_utils import run_bass_kernel_spmd

F32 = mybir.dt.float32
I32 = mybir.dt.int32
BF16 = mybir.dt.bfloat16
AF = mybir.ActivationFunctionType
ALU = mybir.AluOpType
AX = mybir.AxisListType

NT = 1536
NSEG = 6
EPS = 1e-6
SB_BASE = 16512
SB_END = 229376
RING_SLOTS = 3
SCHEDULE = False
MODEL_REPORT = False
SCHED_WINDOW = 2000
PART2_DELAY = 4
D2_FFN = 11
D2_AB = 8
D2_C = 6
NSIDE = 3
SLOT_BYTES = 5632

PV_SPEC = [
    ("norm_g", (4, 4, 8)), ("mod_b", (4, 48)), ("lru_cw", (2, 4, 8)), ("lru_cb", (2, 8)),
    ("lru_bg", (2, 2, 2, 8)), ("lru_lam", (2, 2, 8)), ("cf_cw", (2, 31, 8)), ("cf_cb", (2, 8)),
    ("cf_g", (2, 8)), ("cf_b", (2, 8)), ("c_lng", (2, 16)), ("f_cw", (4, 3, 44)), ("f_cb", (4, 44)),
    ("h0", (2, 2, 8)), ("cond", (8, 2)), ("pm", (1,)),
]
PV_OFF = {}
_o = 0
for _n, _s in PV_SPEC:
    PV_OFF[_n] = (_o, _s)
    _o += int(np.prod(_s))
NPV = _o


def pv_index(name, *idx):
    off, shp = PV_OFF[name]
    flat = 0
    for i, n in zip(idx, shp):
        flat = flat * n + i
    rem = int(np.prod(shp[len(idx):])) if len(idx) < len(shp) else 1
    return off + flat * rem, rem


class View:
    __slots__ = ("ap", "sp", "lo", "hi", "n")

    def __init__(self, ap, sp, lo, hi, n=512):
        self.ap, self.sp, self.lo, self.hi, self.n = ap, sp, lo, hi, n


def rev(v):
    return View(v.ap[:, ::-1], v.sp, v.lo, v.hi, v.n)


class Buf:
    def __init__(self, handle, shape, esz, base, sp):
        self.t = handle
        self.shape = list(shape)
        self.esz = esz
        self.base = base
        self.sp = sp
        st = []
        s = 1
        for n in reversed(self.shape):
            st.append(s)
            s *= n
        self.strides = list(reversed(st))
        self.nbytes = s * esz

    def v(self, *idx, parts=None):
        key = [slice(None) if parts is None else parts]
        lo = 0
        hi = 0
        cnt = 1
        for d, (n, st) in enumerate(zip(self.shape, self.strides)):
            i = idx[d] if d < len(idx) else None
            if i is None:
                i = slice(None)
            if isinstance(i, int):
                a = b = i
            else:
                a, b, step = i.indices(n)
                assert step > 0 and b > a, (i, n)
                cnt *= (b - a - 1) // step + 1
                b = a + ((b - a - 1) // step) * step
            key.append(i)
            lo += a * st
            hi += b * st
        blo = self.base + lo * self.esz
        bhi = self.base + (hi + 1) * self.esz
        if self.sp == "ps":
            blo = blo // 2048 * 2048
            bhi = (bhi + 2047) // 2048 * 2048
        return View(self.t.ap()[tuple(key)], self.sp, blo, bhi, cnt)


class Ins:
    __slots__ = ("eng", "emit", "deps", "order", "flag", "ms", "dma_sem", "dma_val", "key", "gi", "dur", "tbl", "lat", "tag")

    def __init__(self, eng, emit):
        self.eng = eng
        self.emit = emit
        self.deps = set()
        self.order = set()
        self.flag = False
        self.ms = 0
        self.dma_sem = None
        self.dma_val = 0
        self.key = eng
        self.gi = 0
        self.dur = 0.3
        self.tbl = None
        self.lat = 0.0


class Prog:
    ENGS = ("pe", "act", "dve", "pool", "sp")

    def __init__(self):
        self.q = {e: [] for e in self.ENGS}
        self.recs = {"sb": {}, "ps": {}}
        self.dma_count = {}
        self.all = []
        self.tag = ''

    def _track(self, ins, reads, writes):
        seen = set()
        for v, w in [(r, False) for r in reads] + [(x, True) for x in writes]:
            k = (v.sp, v.lo, v.hi, w)
            if k in seen:
                continue
            seen.add(k)
            for key, tok in self.recs[v.sp].items():
                if key[0] < v.hi and v.lo < key[1] and (key[2] or w):
                    if tok is ins:
                        continue
                    if tok.eng == "pe" and ins.eng == "pe" and tok.dma_sem is None:
                        ins.order.add(tok)
                        continue
                    ins.deps.add(tok)
                    tok.flag = True
        for v in writes:
            recs = self.recs[v.sp]
            dead = [k for k in recs if k[0] >= v.lo and k[1] <= v.hi]
            for k in dead:
                del recs[k]
            recs[(v.lo, v.hi, True, ins.key)] = ins
        for v in reads:
            k = (v.lo, v.hi, False, ins.key)
            old = self.recs[v.sp].get(k)
            if old is not None and old is not ins:
                ins.order.add(old)
            self.recs[v.sp][k] = ins

    def _add(self, ins):
        ins.gi = len(self.all)
        ins.tag = self.tag
        self.all.append(ins)
        self.q[ins.eng].append(ins)

    def op(self, eng, reads, writes, emit, dur=0.3, tbl=None):
        ins = Ins(eng, emit)
        ins.dur = dur
        ins.tbl = tbl
        self._track(ins, reads, writes)
        self._add(ins)
        return ins

    def dma(self, queue, sem, pairs, reads, writes):
        n = self.dma_count.get(sem, 0) + len(pairs)
        self.dma_count[sem] = n

        def emit(e, pairs=pairs, sem=sem):
            bi = None
            for o, i in pairs:
                bi = e.dma_start(out=o, in_=i)
                bi.then_inc(sem, 16)
            return None

        ins = Ins(queue, emit)
        ins.dma_sem = sem
        ins.dma_val = 16 * n
        ins.key = ("dma", id(sem))
        nel = sum(v.n for v in (writes or reads))
        ins.dur = 0.6
        ins.lat = 2.0 + nel * 512 / 300e3
        self._track(ins, reads, writes)
        self._add(ins)
        return ins

    def schedule(self):
        import heapq
        N = len(self.all)
        preds = [set() for _ in range(N)]
        for ins in self.all:
            for d in ins.deps:
                preds[ins.gi].add(d.gi)
            for d in ins.order:
                preds[ins.gi].add(d.gi)
        for e in ("pool", "sp"):
            qq = self.q[e]
            for a_, b_ in zip(qq, qq[1:]):
                preds[b_.gi].add(a_.gi)
        succ = [[] for _ in range(N)]
        indeg = [0] * N
        for i in range(N):
            indeg[i] = len(preds[i])
            for p in preds[i]:
                succ[p].append(i)
        blevel = [0.0] * N
        for i in range(N - 1, -1, -1):
            ins = self.all[i]
            m = 0.0
            for sidx in succ[i]:
                if blevel[sidx] > m:
                    m = blevel[sidx]
            blevel[i] = ins.dur + ins.lat + 0.12 + m
        ready_t = [0.0] * N
        ready = {e: [] for e in self.ENGS}
        for ins in self.all:
            if indeg[ins.gi] == 0:
                ready[ins.eng].append(ins.gi)
        eng_free = {e: 0.0 for e in self.ENGS}
        cur_tbl = [None]
        newq = {e: [] for e in self.ENGS}
        done = 0
        WINDOW = SCHED_WINDOW
        low = 0
        sched = [False] * N
        while done < N:
            best = None
            for e in self.ENGS:
                r = ready[e]
                if not r:
                    continue
                tmin = min(ready_t[g] for g in r)
                t_e = max(eng_free[e], tmin)
                pick = None
                for g in r:
                    if ready_t[g] <= t_e + 1e-9 and g <= low + WINDOW:
                        key = (-blevel[g], g)
                        if pick is None or key < pick[0]:
                            pick = (key, g)
                if pick is None:
                    g = min(r)
                    t_e = max(eng_free[e], ready_t[g])
                    pick = (None, g)
                cand = (t_e, pick[1], e)
                if best is None or cand < best:
                    best = cand
            st, gi, e = best
            ready[e].remove(gi)
            ins = self.all[gi]
            if e == "act" and ins.tbl is not None and ins.tbl != cur_tbl[0]:
                st += 1.3
                cur_tbl[0] = ins.tbl
            fin = st + ins.dur
            eng_free[e] = fin
            newq[e].append(ins)
            sched[gi] = True
            while low < N and sched[low]:
                low += 1
            done += 1
            avail = fin + ins.lat + 0.12
            for sidx in succ[gi]:
                if ready_t[sidx] < avail:
                    ready_t[sidx] = avail
                indeg[sidx] -= 1
                if indeg[sidx] == 0:
                    ready[self.all[sidx].eng].append(sidx)
        self.q = newq
        return max(eng_free.values())

    def simulate(self, report=None):
        fin_t = {}
        ptr = {e: 0 for e in self.ENGS}
        eng_free = {e: 0.0 for e in self.ENGS}
        cur_tbl = None
        left = sum(len(q) for q in self.q.values())
        while left:
            progressed = False
            for e in self.ENGS:
                q = self.q[e]
                while ptr[e] < len(q):
                    ins = q[ptr[e]]
                    pr = list(ins.deps) + list(ins.order)
                    if any(p.gi not in fin_t for p in pr):
                        break
                    st = max([eng_free[e]] + [fin_t[p.gi] + p.lat + 0.12 for p in pr])
                    if report is not None and e == "pe" and st - eng_free[e] > report[2] and report[0] <= st <= report[1]:
                        cp = max(pr, key=lambda p: fin_t[p.gi] + p.lat)
                        print(f"  PE idle {eng_free[e]:8.1f}->{st:8.1f} [{ins.tag}] waits on {cp.eng}:{cp.tag} (dur {cp.dur:.2f}, dma={cp.dma_sem is not None})")
                    if e == "act" and ins.tbl is not None and ins.tbl != cur_tbl:
                        st += 1.3
                        cur_tbl = ins.tbl
                    fin_t[ins.gi] = st + ins.dur
                    eng_free[e] = st + ins.dur
                    ptr[e] += 1
                    left -= 1
                    progressed = True
            assert progressed, "deadlock in queue order"
        return max(eng_free.values())


def build(n_layers=4):
    nc = bass.Bass("TRN2", target_bir_lowering=False)

    def dr(name, shape, kind="ExternalInput"):
        return nc.dram_tensor(name, list(shape), F32, kind=kind).ap()

    d_xT = dr("xT", [1024, NT])
    d_pv = dr("pv", [128, NPV])
    d_mask = dr("mask", [128, 80])
    d_ident = dr("ident", [128, 128])
    d_mod_w = dr("mod_w", [4, 24, 128, 2048])
    d_ab_g = dr("ab_g", [2, 4, 128, 2048])
    d_ab_x = dr("ab_x", [2, 8, 128, 1024])
    d_ab_glu = dr("ab_glu", [2, 8, 128, 2048])
    d_ab_out = dr("ab_w_out", [2, 8, 128, 2048])
    d_c_v = dr("c_v", [2, 8, 128, 2048])
    d_c_u = dr("c_u", [2, 8, 128, 2048])
    d_c_out = dr("c_w_out", [2, 8, 128, 2048])
    d_up = dr("ffn_w_up", [4, 22, 128, 2048])
    d_down = dr("ffn_w_down", [4, 8, 128, 2816])
    d_wg = dr("lru_wg", [2, 2, 2, 8, 128, 128])
    d_wst = dr("wst", [2, 128, 8, 128])
    d_cbs = dr("c_bs", [2, 8, 128])
    d_clnb = dr("c_lnb", [2, 2048])
    d_yT = dr("yT", [1024, NT], kind="ExternalOutput")
    d_st = dr("st", [24, 1024], kind="ExternalOutput")

    P = Prog()
    cnt = [0]

    def sbuf(shape, dtype, offset):
        cnt[0] += 1
        esz = 2 if dtype == BF16 else 4
        h = nc.alloc_sbuf_tensor_at(f"b{cnt[0]}", [128] + list(shape), dtype, offset=offset)
        b = Buf(h, shape, esz, offset, "sb")
        assert offset + b.nbytes <= SB_END, (shape, offset)
        return b

    ptr = [SB_BASE]

    def alloc(shape, dtype):
        off = (ptr[0] + 31) // 32 * 32
        b = sbuf(shape, dtype, off)
        ptr[0] = off + b.nbytes
        return b

    def alias(buf, shape, dtype):
        return sbuf(shape, dtype, buf.base)

    X = alloc([8, NT], F32)
    H = alloc([8, NT], BF16)
    RSTD = alloc([NT], F32)
    SQT = alloc([8, 512], BF16)
    SQT3 = alias(SQT, [8, 2, 256], BF16)
    RING = alloc([RING_SLOTS, SLOT_BYTES // 2], BF16)
    PV = alloc([NPV], F32)
    IDF = alloc([128], F32)
    IDENT = alloc([128], BF16)
    ONES = alloc([128], BF16)
    ONE1 = alloc([128], BF16)
    MASK = alloc([5, 16], F32)
    SC = alloc([8, 2], BF16)
    SCF = alias(SC, [16], BF16)
    MODS = [alloc([96], F32), alloc([96], F32)]
    DERS = [alloc([4, 8, 2], F32), alloc([4, 8, 2], F32)]
    TMP = alloc([2, 512], F32)
    TMP3 = alias(TMP, [2, 2, 256], F32)
    SQW = alloc([3, 512], BF16)
    LC = alloc([5, 16], F32)
    BGH = alloc([32], F32)
    ST = alloc([8, 24], F32)
    STT_ = alias(SQT, [1024], F32)
    INIT = alloc([2, 6], F32)
    PHASE = (ptr[0] + 31) // 32 * 32

    ptr[0] = PHASE
    YA = alloc([8, NT], BF16)
    GP = alloc([8, 6, 286], BF16)
    DCONF = alloc([31, 128], BF16)
    TG = alloc([NT], BF16)
    TG3 = alias(TG, [6, 256], BF16)
    WGC = alloc([2, 4, 128], BF16)
    DLC = alloc([1, 4, 128], BF16)
    XBP = alloc([6, 259], BF16)
    HF = alias(RSTD, [NT], F32)
    LT = alloc([3, 3, 512], F32)
    LNT = alias(LT, [3, 512], F32)
    LNT3 = alias(LT, [3, 2, 256], F32)
    XCB2 = alias(SQT, [2, NT], BF16)
    HB = alias(TMP, [2, 512], F32)
    assert ptr[0] <= SB_END, ptr[0]

    ptr[0] = PHASE
    U = alloc([22, NT], BF16)
    ZG = alloc([6, 258], BF16)
    ZV = alloc([6, 258], BF16)
    DF = alloc([2, 3, 128], BF16)
    GT = alloc([3, 512], BF16)
    OT = alloc([3, 512], F32)
    OT3 = alias(OT, [3, 2, 256], F32)
    assert ptr[0] <= SB_END

    ptr[0] = PHASE
    V = alloc([16, 12, 128], BF16)
    US = alias(V, [16, NT], BF16)
    WST = alloc([8, 128], BF16)
    BT = alloc([16, 1, 128], F32)
    LB = alloc([16, 128], F32)
    RB = alloc([8, 128], F32)
    S1T = alloc([3, 512], F32)
    S1T4 = alias(S1T, [3, 4, 128], F32)
    UG = alloc([3, 512], BF16)
    CS1 = alloc([12, 8], F32)
    CS2 = alloc([12, 8], F32)
    CST = alloc([6, 12], F32)
    JUNK = alloc([2, 128], BF16)
    JUNKF = alloc([2, 256], F32)
    JUNKF3 = alias(JUNKF, [2, 2, 128], F32)
    assert ptr[0] <= SB_END

    ptr[0] = PHASE
    ROWV = alloc([16], F32)
    COLV = alloc([64], F32)
    PIDX = alloc([2], F32)
    FREQ = alloc([2], F32)
    PANG = alloc([64], F32)
    PKF = alloc([64], F32)
    PKI = alloc([64], I32)
    PSNR = alloc([16, 1], F32)
    PSNC = alloc([1, 64], F32)
    X3 = alias(X, [8, 24, 64], F32)

    ps_h = nc.alloc_psum_tensor("ps", [128, 8, 512], F32)
    PS = Buf(ps_h, [8, 512], 4, 0, "ps")
    PS3 = Buf(ps_h.reshape([128, 8, 2, 256]), [8, 2, 256], 4, 0, "ps")
    PS4 = Buf(ps_h.reshape([128, 8, 4, 128]), [8, 4, 128], 4, 0, "ps")
    bank_rr = [0]

    bank_pool = [4]

    def bank():
        b = bank_rr[0] % bank_pool[0]
        bank_rr[0] += 1
        return b

    sems = {}
    sem_names = ["pe", "act", "dve", "poolc", "ld", "x", "wg", "c", "out"] + [f"r{i}" for i in range(RING_SLOTS)]
    import contextlib
    stack = contextlib.ExitStack()
    for n in sem_names:
        sems[n] = stack.enter_context(nc.semaphore(n))

    def col(buf, i):
        return buf.v(slice(i, i + 1))

    def pvc(name, *idx):
        i, rem = pv_index(name, *idx)
        assert rem == 1
        return col(PV, i)

    def pvr(name, *idx):
        i, rem = pv_index(name, *idx)
        return PV.v(slice(i, i + rem))

    def isv(x):
        return isinstance(x, View)

    ACT_TBL = {AF.Sin: "T", AF.Tanh: "A", AF.Exp: "A", AF.Sqrt: "S", AF.Gelu_apprx_tanh: "G", AF.Silu: "L", AF.Ln: "N"}

    def dve_cost(v):
        return 0.08 + v.n / 960.0

    def ACT(out, in_, func, bias=0.0, scale=1.0, accum=None):
        reads = [in_] + [x for x in (bias, scale) if isv(x)]
        kw = dict(bias=bias.ap if isv(bias) else float(bias), scale=scale.ap if isv(scale) else float(scale))
        if accum is not None:
            kw["accum_out"] = accum.ap
        tbl = ACT_TBL.get(func)
        P.op("act", reads, [out] + ([accum] if accum is not None else []),
             lambda e: e.activation(out=out.ap, in_=in_.ap, func=func, **kw),
             dur=0.2 + out.n / 1200.0 + (0.1 if accum is not None else 0.0), tbl=tbl)

    def TT(out, a, b, op):
        P.op("dve", [a, b], [out], lambda e: e.tensor_tensor(out=out.ap, in0=a.ap, in1=b.ap, op=op), dur=dve_cost(out))

    def TS(out, a, s1, s2, op0, op1=None):
        reads = [a] + [x for x in (s1, s2) if isv(x)]
        a1 = s1.ap if isv(s1) else float(s1)
        a2 = (s2.ap if isv(s2) else (None if s2 is None else float(s2)))
        if op1 is None:
            P.op("dve", reads, [out], lambda e: e.tensor_scalar(out=out.ap, in0=a.ap, scalar1=a1, scalar2=None, op0=op0), dur=dve_cost(out))
        else:
            P.op("dve", reads, [out], lambda e: e.tensor_scalar(out=out.ap, in0=a.ap, scalar1=a1, scalar2=a2, op0=op0, op1=op1), dur=dve_cost(out))

    def STT(out, a, s, b, op0, op1):
        reads = [a, b] + ([s] if isv(s) else [])
        sv = s.ap if isv(s) else float(s)
        P.op("dve", reads, [out], lambda e: e.scalar_tensor_tensor(out=out.ap, in0=a.ap, scalar=sv, in1=b.ap, op0=op0, op1=op1), dur=dve_cost(out))

    def SCAN(out, a, b, init):
        reads = [a, b] + ([init] if isv(init) else [])
        iv = init.ap if isv(init) else float(init)
        P.op("dve", reads, [out], lambda e: e.tensor_tensor_scan(out=out.ap, data0=a.ap, data1=b.ap, initial=iv, op0=ALU.mult, op1=ALU.add), dur=0.08 + 2 * out.n / 960.0)

    def COPY(out, in_):
        P.op("dve", [in_], [out], lambda e: e.tensor_copy(out=out.ap, in_=in_.ap), dur=dve_cost(out))

    def RECIP(out, in_):
        P.op("dve", [in_], [out], lambda e: e.reciprocal(out=out.ap, in_=in_.ap), dur=dve_cost(out))

    def MEMSET(out, val):
        P.op("dve", [], [out], lambda e: e.memset(out.ap, float(val)), dur=dve_cost(out))

    def MM(out, pairs, start=True, stop=True):
        reads = [v for pr in pairs for v in pr]

        def emit(e):
            n = len(pairs)
            bi = None
            for i, (l, r) in enumerate(pairs):
                bi = e.matmul(out.ap, l.ap, r.ap, start=(start and i == 0), stop=(stop and i == n - 1))
            return bi
        P.op("pe", reads, [out], emit, dur=0.03 + sum(max(64, r.n) for _, r in pairs) / 2400.0)

    def MMI(outs_pairs, whole):
        reads = [v for (_, l, r) in outs_pairs for v in (l, r)]

        def emit(e):
            bi = None
            for o, l, r in outs_pairs:
                bi = e.matmul(o.ap, l.ap, r.ap, start=True, stop=True)
            return bi
        P.op("pe", reads, [whole], emit, dur=0.03 + sum(max(64, r.n) for _, _l, r in outs_pairs) / 2400.0)

    def ts(tt):
        return slice(tt * 512, (tt + 1) * 512)

    def sg(tt):
        return slice(2 * tt, 2 * tt + 2)

    ring_i = [0]
    slot_bufs = {}

    def wtile(src, kc, fw):
        slot = ring_i[0] % RING_SLOTS
        ring_i[0] += 1
        tot = kc * fw
        assert tot * 2 <= SLOT_BYTES
        key = (slot, kc, fw)
        if key not in slot_bufs:
            slot_bufs[key] = (sbuf([kc, fw], BF16, RING.base + slot * SLOT_BYTES), sbuf([tot], BF16, RING.base + slot * SLOT_BYTES))
        sb, flat = slot_bufs[key]
        if tot <= 2048:
            pairs = [(flat.v().ap, src)]
        else:
            assert tot % 2 == 0
            pairs = [(flat.v().ap.rearrange("p (a b) -> p a b", a=2), src.rearrange("p (a b) -> p a b", a=2))]
        P.dma("pool", sems[f"r{slot}"], pairs, [], [RING.v(slot)])
        return sb

    def sp_load(out_view, in_ap, sem="ld"):
        P.dma("sp", sems[sem], [(out_view.ap, in_ap)], [], [out_view])

    sp_load(PV.v(), d_pv)
    sp_load(MASK.v(), d_mask.rearrange("p (a b) -> p a b", b=16))
    sp_load(IDF.v(), d_ident)
    for kc in range(8):
        sp_load(X.v(kc), d_xT[kc * 128:(kc + 1) * 128, :], sem="x")
    COPY(IDENT.v(), IDF.v())
    MEMSET(ONES.v(), 1.0 / 1024.0)
    MEMSET(ONE1.v(), 1.0)
    MEMSET(ST.v(), 0.0)
    MEMSET(INIT.v(), 0.0)
    P.op("pool", [], [ROWV.v()], lambda e: e.iota(ROWV.v().ap, [[1, 16]], base=0, channel_multiplier=0, allow_small_or_imprecise_dtypes=True))
    P.op("pool", [], [COLV.v()], lambda e: e.iota(COLV.v().ap, [[1, 64]], base=0, channel_multiplier=0, allow_small_or_imprecise_dtypes=True))
    P.op("pool", [], [PIDX.v()], lambda e: e.iota(PIDX.v().ap, [[128, 2]], base=0, channel_multiplier=1, allow_small_or_imprecise_dtypes=True))
    ACT(FREQ.v(), PIDX.v(), AF.Exp, scale=-math.log(10000.0) / 256.0)
    TWO_PI = 2.0 * math.pi
    for kc in range(8):
        is_row = kc < 4
        n = 16 if is_row else 64
        src = ROWV.v() if is_row else COLV.v()
        phase = 0.0 if (kc % 4) < 2 else math.pi / 2.0
        ang, kf, ki = PANG.v(slice(0, n)), PKF.v(slice(0, n)), PKI.v(slice(0, n))
        TS(ang, src, col(FREQ, kc % 2), phase, ALU.mult, ALU.add)
        TS(kf, ang, 1.0 / TWO_PI, None, ALU.mult)
        COPY(ki, kf)
        COPY(kf, ki)
        STT(ang, kf, -TWO_PI, ang, ALU.mult, ALU.add)
        if is_row:
            flat = alias(PSNR, [16], F32).v()
            bview = PSNR.v()
        else:
            flat = alias(PSNC, [64], F32).v()
            bview = PSNC.v()
        ACT(flat, ang, AF.Sin)
        TS(flat, flat, pvc("pm", 0), None, ALU.mult)
        bb = View(bview.ap.broadcast_to([128, 16, 64]), bview.sp, bview.lo, bview.hi, 1024)
        TT(X3.v(kc, slice(0, 16)), X3.v(kc, slice(0, 16)), bb, ALU.add)
    ACT(SCF.v(), pvr("cond"), AF.Silu)

    def mod_steps(l, mb, dst):
        steps = []
        for t in range(24):
            def step(t=t):
                sl = wtile(d_mod_w[l, t], 8, 256)
                for f in range(2):
                    fc = 2 * t + f
                    MM(PS.v(mb, slice(2 * fc, 2 * fc + 2)),
                       [(sl.v(kc, slice(128 * f, 128 * f + 128)), SC.v(kc)) for kc in range(8)])
                if t % 4 == 3:
                    part = t // 4
                    off, _ = pv_index("mod_b", l)
                    for ci in range(2):
                        sel = slice(16 * part + ci, 16 * (part + 1), 2)
                        TT(dst.v(sel), PS.v(mb, sel), PV.v(slice(off + 8 * part, off + 8 * part + 8)), ALU.add)
            steps.append(step)
        return steps

    def derive(l, mod, DER, which=(0, 1, 2, 3)):
        for ci in range(2):
            def m(part):
                return mod.v(slice(2 * 8 * part + ci, 2 * 8 * (part + 1), 2))
            if 0 in which:
                STT(DER.v(0, None, ci), m(1), 1.0, pvr("norm_g", l, 0), ALU.add, ALU.mult)
            if 1 in which:
                TT(DER.v(1, None, ci), m(2), pvr("norm_g", l, 1), ALU.mult)
            if 2 in which:
                STT(DER.v(2, None, ci), m(4), 1.0, pvr("norm_g", l, 2), ALU.add, ALU.mult)
            if 3 in which:
                TT(DER.v(3, None, ci), m(5), pvr("norm_g", l, 3), ALU.mult)

    def mcol(mod, part, kc, ci):
        i = 2 * (8 * part + kc) + ci
        return col(mod, i)

    def norm_mod(der, which_a, mod, part_b, between=None):
        for tt in range(3):
            ci = 0 if tt < 2 else 1
            for kc in range(8):
                ACT(SQT.v(kc), X.v(kc, ts(tt)), AF.Square)
            MM(PS.v(5 + tt), [(ONES.v(), SQT.v(kc)) for kc in range(8)])
            ACT(RSTD.v(ts(tt)), PS.v(5 + tt), AF.Sqrt, bias=EPS)
            RECIP(RSTD.v(ts(tt)), RSTD.v(ts(tt)))
            for kc in range(8):
                t = TMP.v(kc % 2)
                TT(t, X.v(kc, ts(tt)), RSTD.v(ts(tt)), ALU.mult)
                ACT(H.v(kc, ts(tt)), t, AF.Identity, scale=der.v(which_a, slice(kc, kc + 1), ci), bias=mcol(mod, part_b, kc, ci))
            if between is not None:
                between()

    GSIDE = []

    def run_gside(n):
        for _ in range(n):
            if GSIDE:
                GSIDE.pop(0)()

    def drain_gside():
        run_gside(len(GSIDE))

    def out_proj(w2d, nk, rhs_fn, three_d, der_g, which_g, nxt, pre=None, fill=None, d2=0, carry=False):
        cnt_ = [0]
        drain_gside()
        side = GSIDE

        def run_side(n):
            for _ in range(n):
                if side:
                    side.pop(0)()

        def groups(tts, nside):
            P.tag = f'outproj.g{tts}'
            pend = []

            def stat(item):
                sq, fc, tt = item
                MM(PS.v(5 + tt), [(ONES.v(), sq)], start=(fc == 0), stop=(fc == 7))
            for fc in range(8):
                sl = wtile(w2d[fc], nk, 128)
                for tt in tts:
                    b = bank()
                    o = PS3.v(b) if three_d else PS.v(b)
                    MM(o, [(sl.v(kc), rhs_fn(kc, tt)) for kc in range(nk)])
                    ACT(H.v(fc, ts(tt)), PS.v(b), AF.Copy)
                    sq = SQW.v(cnt_[0] % 3)
                    cnt_[0] += 1
                    ACT(sq, PS.v(b), AF.Square)
                    pend.append((sq, fc, tt))
                    if len(pend) > 2:
                        stat(pend.pop(0))
                    run_side(nside)
            while pend:
                stat(pend.pop(0))

        def tail_steps(tt):
            ci = 0 if tt < 2 else 1

            def s0():
                P.tag = f'tail{tt}'
                ACT(RSTD.v(ts(tt)), PS.v(5 + tt), AF.Sqrt, bias=EPS)
                RECIP(RSTD.v(ts(tt)), RSTD.v(ts(tt)))

            def sk(kc):
                P.tag = f'tail{tt}'
                if kc < 8:
                    t = TMP.v(kc % 2)
                    STT(t, H.v(kc, ts(tt)), der_g.v(which_g, slice(kc, kc + 1), ci), RSTD.v(ts(tt)), ALU.mult, ALU.mult)
                    TT(X.v(kc, ts(tt)), X.v(kc, ts(tt)), t, ALU.add)
                if nxt is not None and kc >= 1:
                    ACT(SQT.v(kc - 1), X.v(kc - 1, ts(tt)), AF.Square)
            return [s0] + [lambda kc=kc: sk(kc) for kc in range(9 if nxt is not None else 8)]

        def part2_steps(tt):
            if nxt is None:
                return []
            der_n, which_a, mod_n, part_b = nxt
            ci = 0 if tt < 2 else 1

            def s0():
                P.tag = f'part2_{tt}'
                MM(PS.v(5 + tt), [(ONES.v(), SQT.v(kc)) for kc in range(8)])
                ACT(RSTD.v(ts(tt)), PS.v(5 + tt), AF.Sqrt, bias=EPS)
                RECIP(RSTD.v(ts(tt)), RSTD.v(ts(tt)))

            def sk(kc):
                P.tag = f'part2_{tt}'
                if kc < 8:
                    TT(TMP.v(kc % 2), X.v(kc, ts(tt)), RSTD.v(ts(tt)), ALU.mult)
                if kc >= 1:
                    k1 = kc - 1
                    ACT(H.v(k1, ts(tt)), TMP.v(k1 % 2), AF.Identity, scale=der_n.v(which_a, slice(k1, k1 + 1), ci), bias=mcol(mod_n, part_b, k1, ci))
            return [s0] + [lambda kc=kc: sk(kc) for kc in range(9)]

        if pre is not None:
            for st_ in pre(0) + pre(1):
                st_()
            side.extend(pre(2))
        for tt in range(3):
            groups([tt], NSIDE)
            side.extend(tail_steps(tt))
            if nxt is not None:
                side.extend([(lambda: None)] * (PART2_DELAY if tt < 2 else (d2 if carry else 0)))
            side.extend(part2_steps(tt))
        if fill is not None:
            fill()
        if nxt is None or not carry:
            run_side(len(side))

    def interleave(*gens):
        gens = list(gens)
        while gens:
            for g in list(gens):
                try:
                    next(g)
                except StopIteration:
                    gens.remove(g)

    def POOL_TS(out, a, s1, op0):
        P.op("pool", [a, s1], [out], lambda e: e.tensor_scalar(out=out.ap, in0=a.ap, scalar1=s1.ap, scalar2=None, op0=op0))

    def mixer_ab(jj, der, nxt, fill):
        YA3 = ya3[0]
        sls = {}

        def gate_group(j, f, tt):
            fc = 2 * j + f
            b = bank()
            MM(PS.v(b), [(sls[j].v(kc, slice(128 * f, 128 * f + 128)), H.v(kc, ts(tt))) for kc in range(8)])
            ACT(YA.v(fc, ts(tt)), PS.v(b), AF.Gelu_apprx_tanh)
        for j in range(3):
            P.tag = f'gate_br{j}'
            sls[j] = wtile(d_ab_g[jj, j], 8, 256)
            for f in range(2):
                for tt in range(2):
                    gate_group(j, f, tt)
                    run_gside(2)
        drain_gside()
        for j in range(3):
            for f in range(2):
                gate_group(j, f, 2)
        P.tag = 'gate_br3'
        sls[3] = wtile(d_ab_g[jj, 3], 8, 256)
        for f in range(2):
            for tt in range(3):
                gate_group(3, f, tt)
        lam = pvr("lru_lam", jj)
        ACT(LC.v(0), lam, AF.Exp, scale=-1.0)
        ACT(LC.v(0), LC.v(0), AF.Ln, bias=1.0)
        TS(LC.v(1), LC.v(0), -4.0, None, ALU.mult)
        TS(BGH.v(), pvr("lru_bg", jj), 0.5, None, ALU.mult)
        TS(pvr("cf_cw", jj), pvr("cf_cw", jj), 0.5, None, ALU.mult)
        MEMSET(XBP.v(0, slice(0, 2)), 0.0)
        MEMSET(XBP.v(5, slice(258, 259)), 0.0)
        MEMSET(GP.v(None, 0, slice(0, 15)), 0.0)
        MEMSET(GP.v(None, 5, slice(271, 286)), 0.0)
        bank_pool[0] = 8

        def lru_pre(c):
            P.tag = f'lru_pre{c}'
            sl = wtile(d_ab_x[jj, c], 8, 128)
            wq = c % 2
            P.dma("pool", sems["wg"], [(WGC.v(wq).ap, d_wg[jj][:, :, c].rearrange("d g i j -> i (d g) j"))], [], [WGC.v(wq)])
            for k in range(4):
                TS(DLC.v(0, k), IDENT.v(), pvc("lru_cw", jj, k, c), None, ALU.mult)
            for tt in range(3):
                b = bank()
                MM(PS.v(b), [(sl.v(kc), H.v(kc, ts(tt))) for kc in range(8)])
                ACT(XBP.v(sg(tt), slice(2, 258)), PS3.v(b), AF.Copy)
            TT(XBP.v(slice(1, 6), slice(0, 2)), XBP.v(slice(0, 5), slice(256, 258)), MASK.v(None, slice(0, 2)), ALU.mult)
            TT(XBP.v(slice(0, 5), slice(258, 259)), XBP.v(slice(1, 6), slice(2, 3)), MASK.v(None, slice(0, 1)), ALU.mult)
            for tt in range(3):
                b = bank()
                MM(PS3.v(b), [(DLC.v(0, k), XBP.v(sg(tt), slice(k, k + 256))) for k in range(4)])
                ACT(XCB2.v(wq, ts(tt)), PS.v(b), AF.Identity, bias=pvc("lru_cb", jj, c))

        def lru_chunk(c):
            wq = c % 2
            if c == 0:
                lru_pre(0)
                yield
            for d in range(2):
                order = [0, 1, 2] if d == 0 else [2, 1, 0]
                chc = LC.v(1, slice(d * 8 + c, d * 8 + c + 1))
                if d == 1 and c < 7:
                    lru_pre(c + 1)
                P.tag = f'lru{c}.d{d}s1'
                for q, tt in enumerate(order):
                    Aa, TI, S = (LT.v(q, z) for z in range(3))
                    br = bank()
                    MM(PS.v(br), [(WGC.v(wq, d * 2 + 0), XCB2.v(wq, ts(tt)))])
                    bi_ = bank()
                    MM(PS.v(bi_), [(WGC.v(wq, d * 2 + 1), XCB2.v(wq, ts(tt)))])
                    ACT(Aa, PS.v(br), AF.Tanh, scale=0.5, bias=col(BGH, (d * 2 + 0) * 8 + c))
                    ACT(TI, PS.v(bi_), AF.Tanh, scale=0.5, bias=col(BGH, (d * 2 + 1) * 8 + c))
                    ACT(Aa, Aa, AF.Exp, scale=chc, bias=chc)
                    STT(S, Aa, -1.0, Aa, ALU.mult, ALU.mult)
                    TS(S, S, 1.0, 0.0, ALU.add, ALU.max)
                    STT(TI, TI, 1.0, XCB2.v(wq, ts(tt)), ALU.add, ALU.mult)
                    if d == 0:
                        TT(LT.v(q, 0, slice(256, 257)), LT.v(q, 0, slice(256, 257)), MASK.v(2 * tt, slice(0, 1)), ALU.mult)
                        if tt == 1:
                            TT(LT.v(q, 0, slice(0, 1)), LT.v(q, 0, slice(0, 1)), MASK.v(1, slice(0, 1)), ALU.mult)
                    else:
                        TT(LT.v(q, 0, slice(255, 256)), LT.v(q, 0, slice(255, 256)), MASK.v(2 * tt, slice(0, 1)), ALU.mult)
                        if tt == 0:
                            TT(LT.v(q, 0, slice(511, 512)), LT.v(q, 0, slice(511, 512)), MASK.v(1, slice(0, 1)), ALU.mult)
                yield
                P.tag = f'lru{c}.d{d}s2'
                for q, tt in enumerate(order):
                    Aa, TI, S = (LT.v(q, z) for z in range(3))
                    ACT(S, S, AF.Sqrt, scale=0.25)
                for q, tt in enumerate(order):
                    Aa, TI, S = (LT.v(q, z) for z in range(3))
                    TT(TI, TI, S, ALU.mult)
                    r0 = (jj * 2 + d) * 6
                    if d == 0:
                        init = pvc("h0", jj, 0, c) if tt == 0 else (HF.v(slice(511, 512)) if tt == 1 else 0.0)
                        SCAN(HF.v(ts(tt)), Aa, TI, init)
                        if tt == 2:
                            COPY(ST.v(c, slice(r0, r0 + 6)), HF.v(slice(255, NT, 256)))
                    else:
                        hb = HB.v(q % 2)
                        init = 0.0 if tt == 2 else (pvc("h0", jj, 1, c) if tt == 1 else ST.v(c, slice(r0 + 2, r0 + 3)))
                        SCAN(rev(hb), rev(Aa), rev(TI), init)
                        COPY(ST.v(c, slice(r0 + 2 * tt, r0 + 2 * tt + 2)), HB.v(q % 2, slice(0, 512, 256)))
                        TT(hb, hb, HF.v(ts(tt)), ALU.add)
                        TT(YA.v(c, ts(tt)), YA.v(c, ts(tt)), hb, ALU.mult)
                yield

        def dconf_builds(c):
            for k in range(31):
                if k % 2 == 0:
                    ACT(DCONF.v(k), IDENT.v(), AF.Identity, scale=pvc("cf_cw", jj, k, c))
                else:
                    TS(DCONF.v(k), IDENT.v(), pvc("cf_cw", jj, k, c), None, ALU.mult)

        def conf_chunk(c):
            P.tag = f'conf{c}.glub'
            sl = wtile(d_ab_glu[jj, c], 8, 256)
            if c == 0:
                dconf_builds(0)
            for tt in range(3):
                b = bank()
                MM(PS.v(b), [(sl.v(kc, slice(128, 256)), H.v(kc, ts(tt))) for kc in range(8)])
                ACT(TG.v(ts(tt)), PS.v(b), AF.Tanh, scale=0.5)
            P.tag = f'conf{c}.glua'
            for tt in range(3):
                b = bank()
                MM(PS.v(b), [(sl.v(kc, slice(0, 128)), H.v(kc, ts(tt))) for kc in range(8)])
                STT(GP.v(c, sg(tt), slice(15, 271)), TG3.v(sg(tt)), 1.0, PS3.v(b), ALU.add, ALU.mult)
            TT(GP.v(c, slice(1, 6), slice(0, 15)), GP.v(c, slice(0, 5), slice(256, 271)), MASK.v(None, slice(0, 15)), ALU.mult)
            TT(GP.v(c, slice(0, 5), slice(271, 286)), GP.v(c, slice(1, 6), slice(15, 30)), MASK.v(None, slice(0, 15)), ALU.mult)
            yield
            P.tag = f'conf{c}.conv'
            for tt in range(3):
                b = bank()
                MM(PS3.v(b), [(DCONF.v(k), GP.v(c, sg(tt), slice(k, k + 256))) for k in range(31)])
                ACT(GP.v(c, sg(tt), slice(15, 271)), PS3.v(b), AF.Identity, bias=pvc("cf_cb", jj, c))
                if tt == 2 and c < 7:
                    P.tag = f'conf{c}.builds'
                    dconf_builds(c + 1)
                yield

        for c in range(8):
            interleave(lru_chunk(c), conf_chunk(c))
        bank_pool[0] = 4

        def ln_tile(tt):
            def s0():
                for c in range(8):
                    ACT(SQT3.v(c), GP.v(c, sg(tt), slice(15, 271)), AF.Square)
                bm = bank()
                MM(PS3.v(bm), [(ONES.v(), GP.v(c, sg(tt), slice(15, 271))) for c in range(8)])
                be = bank()
                MM(PS.v(be), [(ONES.v(), SQT.v(c)) for c in range(8)])
                ACT(LNT.v(0), PS.v(bm), AF.Copy)
                TT(LNT.v(1), LNT.v(0), LNT.v(0), ALU.mult)
                TT(LNT.v(1), PS.v(be), LNT.v(1), ALU.subtract)
                TS(LNT.v(1), LNT.v(1), 0.0, None, ALU.max)
                ACT(LNT.v(1), LNT.v(1), AF.Sqrt, bias=EPS)
                RECIP(LNT.v(1), LNT.v(1))
                TT(LNT.v(2), LNT.v(0), LNT.v(1), ALU.mult)

            def sk(c):
                t = TMP3.v(c % 2)
                g = GP.v(c, sg(tt), slice(15, 271))
                TT(t, g, LNT3.v(1), ALU.mult)
                TT(t, t, LNT3.v(2), ALU.subtract)
                ACT(g, t, AF.Silu, scale=pvc("cf_g", jj, c), bias=pvc("cf_b", jj, c))
            return [s0] + [lambda c=c: sk(c) for c in range(8)]

        def rhs3(kc, tt):
            return YA3.v(kc, sg(tt)) if kc < 8 else GP.v(kc - 8, sg(tt), slice(15, 271))
        out_proj(d_ab_out[jj], 16, rhs3, True, der, 1, nxt, pre=ln_tile, fill=fill, d2=D2_FFN)

    ya3 = [alias(YA, [8, 6, 256], BF16)]

    def mixer_c(jj, der, nxt, fill):
        P.dma("pool", sems["wg"], [(WST.v().ap, d_wst[jj])], [], [WST.v()])
        MEMSET(LB.v(parts=slice(0, 2)), 1.0)
        P.dma("sp", sems["c"], [(LB.v(parts=slice(0, 1)).ap, d_clnb[jj:jj + 1, :].rearrange("o (a b) -> o a b", b=128))], [], [LB.v()])
        P.dma("sp", sems["c"], [(RB.v(parts=slice(1, 2)).ap, d_cbs[jj:jj + 1, :, :])], [], [RB.v()])
        vsl = {}

        def v_group(sI, m):
            sl = vsl[sI]
            b = bank()
            MM(PS.v(b, slice(0, 256)), [(H.v(kc, slice(128 * m, 128 * m + 128)), sl.v(kc)) for kc in range(8)])
            vv = V.v(slice(2 * sI, 2 * sI + 2), m)
            ACT(vv, PS4.v(b, slice(0, 2)), AF.Gelu_apprx_tanh, accum=CS1.v(m, slice(sI, sI + 1)))
            jf = JUNKF.v((sI * 12 + m) % 2)
            TT(JUNKF3.v((sI * 12 + m) % 2), vv, vv, ALU.mult)
            P.op("dve", [jf], [CS2.v(m, slice(sI, sI + 1))], lambda e, jf=jf, o=CS2.v(m, slice(sI, sI + 1)): e.reduce_sum(out=o.ap, in_=jf.ap, axis=AX.X), dur=0.35)
        for sI in range(3):
            P.tag = f'c_v{sI}'
            vsl[sI] = wtile(d_c_v[jj, sI], 8, 256)
            for m in range(8):
                v_group(sI, m)
                run_gside(1)
        drain_gside()
        for sI in range(3):
            for m in range(8, 12):
                v_group(sI, m)
        for sI in range(3, 8):
            P.tag = f'c_v{sI}'
            vsl[sI] = wtile(d_c_v[jj, sI], 8, 256)
            for m in range(12):
                v_group(sI, m)
        P.op("dve", [CS1.v()], [CST.v(0)], lambda e: e.reduce_sum(out=CST.v(0).ap, in_=CS1.v().ap, axis=AX.X))
        P.op("dve", [CS2.v()], [CST.v(1)], lambda e: e.reduce_sum(out=CST.v(1).ap, in_=CS2.v().ap, axis=AX.X))
        TS(CST.v(2), CST.v(0), 1.0 / 2048.0, None, ALU.mult)
        TT(CST.v(3), CST.v(2), CST.v(2), ALU.mult)
        STT(CST.v(4), CST.v(1), 1.0 / 2048.0, CST.v(3), ALU.mult, ALU.subtract)
        TS(CST.v(4), CST.v(4), 0.0, None, ALU.max)
        ACT(CST.v(4), CST.v(4), AF.Sqrt, bias=EPS)
        RECIP(CST.v(4), CST.v(4))
        STT(CST.v(5), CST.v(2), -1.0, CST.v(4), ALU.mult, ALU.mult)
        for m in range(12):
            if m % 2 == 0:
                TS(V.v(None, m), V.v(None, m), CST.v(4, slice(m, m + 1)), CST.v(5, slice(m, m + 1)), ALU.mult, ALU.add)
            else:
                ACT(V.v(None, m), V.v(None, m), AF.Identity, scale=CST.v(4, slice(m, m + 1)), bias=CST.v(5, slice(m, m + 1)))
        for h in range(8):
            b = bank()
            MM(PS.v(b, slice(0, 128), parts=slice(0, 1)), [(ONE1.v(slice(0, 1)), WST.v(h))])
            ACT(RB.v(h, parts=slice(0, 1)), PS.v(b, slice(0, 128), parts=slice(0, 1)), AF.Copy)
        for dc in range(16):
            b = bank()
            MM(PS.v(b, slice(0, 128)), [(LB.v(dc, parts=slice(0, 2)), RB.v(dc // 2, parts=slice(0, 2)))])
            ACT(BT.v(dc, 0), PS.v(b, slice(0, 128)), AF.Copy)
        i = 0
        for h in range(8):
            P.tag = f'c_u{h}'
            sl = wtile(d_c_u[jj, h], 8, 256)
            for f in range(2):
                dc = 2 * h + f
                for tt in range(3):
                    b = bank()
                    MMI([(PS.v(b, slice(128 * n, 128 * n + 128)), V.v(dc, 4 * tt + n), WST.v(h)) for n in range(4)], PS.v(b))
                    s1 = S1T4.v(i % 3)
                    bt = BT.v(dc)
                    btb = View(bt.ap.broadcast_to([128, 4, 128]), bt.sp, bt.lo, bt.hi, 512)
                    STT(s1, PS4.v(b), pvc("c_lng", jj, dc), btb, ALU.mult, ALU.add)
                    b2 = bank()
                    MM(PS.v(b2), [(sl.v(kc, slice(128 * f, 128 * f + 128)), H.v(kc, ts(tt))) for kc in range(8)])
                    ug = UG.v(i % 3)
                    ACT(ug, PS.v(b2), AF.Gelu_apprx_tanh)
                    TT(US.v(dc, ts(tt)), ug, S1T.v(i % 3), ALU.mult)
                    i += 1
        out_proj(d_c_out[jj], 16, lambda kc, tt: US.v(kc, ts(tt)), False, der, 1, nxt, fill=fill, d2=D2_FFN)

    def ffn(l, extra_steps, der, nxt_fn):
        for Z in (ZG, ZV):
            MEMSET(Z.v(0, slice(0, 1)), 0.0)
            MEMSET(Z.v(5, slice(257, 258)), 0.0)
        for j in range(22):
            P.tag = f'ffn_up{j}'
            sl = wtile(d_up[l, j], 8, 256)
            ds_ = j % 2
            for k in range(3):
                TS(DF.v(ds_, k), IDENT.v(), pvc("f_cw", l, k, j), None, ALU.mult)
            def up_gate(tt):
                b = bank()
                MM(PS.v(b), [(sl.v(kc, slice(0, 128)), H.v(kc, ts(tt))) for kc in range(8)])
                ACT(ZG.v(sg(tt), slice(1, 257)), PS3.v(b), AF.Copy)

            def up_val(tt):
                b = bank()
                MM(PS.v(b), [(sl.v(kc, slice(128, 256)), H.v(kc, ts(tt))) for kc in range(8)])
                ACT(ZV.v(sg(tt), slice(1, 257)), PS3.v(b), AF.Copy)
                ACT(OT3.v(tt), PS3.v(b), AF.Identity, scale=pvc("f_cw", l, 1, 22 + j), bias=pvc("f_cb", l, 22 + j))
            if j == 0:
                for fn, tt in ((up_gate, 0), (up_gate, 1), (up_val, 0), (up_val, 1)):
                    fn(tt)
                    run_gside(5)
                drain_gside()
                up_gate(2)
                up_val(2)
            else:
                for tt in range(3):
                    up_gate(tt)
                for tt in range(3):
                    up_val(tt)
            TT(ZG.v(slice(1, 6), slice(0, 1)), ZG.v(slice(0, 5), slice(256, 257)), MASK.v(None, slice(0, 1)), ALU.mult)
            TT(ZG.v(slice(0, 5), slice(257, 258)), ZG.v(slice(1, 6), slice(1, 2)), MASK.v(None, slice(0, 1)), ALU.mult)
            TT(ZV.v(slice(1, 6), slice(0, 1)), ZV.v(slice(0, 5), slice(256, 257)), MASK.v(None, slice(0, 1)), ALU.mult)
            TT(ZV.v(slice(0, 5), slice(257, 258)), ZV.v(slice(1, 6), slice(1, 2)), MASK.v(None, slice(0, 1)), ALU.mult)
            for tt in range(3):
                b = bank()
                MM(PS3.v(b), [(DF.v(ds_, k), ZG.v(sg(tt), slice(k, k + 256))) for k in range(3)])
                ACT(GT.v(tt), PS.v(b), AF.Gelu_apprx_tanh, bias=pvc("f_cb", l, j))
            for tt in range(3):
                o = OT3.v(tt)
                STT(o, ZV.v(sg(tt), slice(0, 256)), pvc("f_cw", l, 0, 22 + j), o, ALU.mult, ALU.add)
                STT(o, ZV.v(sg(tt), slice(2, 258)), pvc("f_cw", l, 2, 22 + j), o, ALU.mult, ALU.add)
                TT(U.v(j, ts(tt)), OT.v(tt), GT.v(tt), ALU.mult)
            if extra_steps and j % 4 != 3:
                extra_steps.pop(0)()
        while extra_steps:
            extra_steps.pop(0)()
        out_proj(d_down[l], 22, lambda kc, tt: U.v(kc, ts(tt)), False, der, 3, nxt_fn(), d2=(D2_AB if (l + 1) % 2 == 0 else D2_C), carry=True)

    steps0 = mod_steps(0, 4, MODS[0])
    for _ in range(8):
        steps0.pop(0)()
    derive(0, MODS[0], DERS[0], which=(0,))

    def between0():
        for _ in range(6):
            if steps0:
                steps0.pop(0)()
    norm_mod(DERS[0], 0, MODS[0], 0, between=between0)
    while steps0:
        steps0.pop(0)()
    derive(0, MODS[0], DERS[0], which=(1, 2, 3))
    for l in range(n_layers):
        mod = MODS[l % 2]
        der = DERS[l % 2]
        last = (l + 1 == n_layers)
        steps = mod_steps(l + 1, 4, MODS[(l + 1) % 2]) if not last else []

        def fill(steps=steps):
            for _ in range(8):
                if steps:
                    steps.pop(0)()
        if l % 2 == 0:
            mixer_ab(l // 2, der, (der, 2, mod, 3), fill)
        else:
            mixer_c(l // 2, der, (der, 2, mod, 3), fill)

        def nxt_fn(l=l, last=last):
            if last:
                return None
            derive(l + 1, MODS[(l + 1) % 2], DERS[(l + 1) % 2])
            return (DERS[(l + 1) % 2], 0, MODS[(l + 1) % 2], 0)
        ffn(l, steps, der, nxt_fn)

    drain_gside()
    for kc in range(8):
        P.dma("sp", sems["out"], [(d_yT[kc * 128:(kc + 1) * 128, :], X.v(kc).ap)], [X.v(kc)], [])
    for c in range(8):
        b = bank()
        o = PS.v(b, slice(0, 128), parts=slice(0, 24))
        P.op("pe", [ST.v(c), IDF.v()], [o], lambda e, o=o, c=c: e.transpose(out=o.ap, in_=ST.v(c).ap, identity=IDF.v().ap))
        ACT(STT_.v(slice(128 * c, 128 * c + 128), parts=slice(0, 24)), o, AF.Copy)
    P.dma("sp", sems["out"], [(d_st, STT_.v(parts=slice(0, 24)).ap)], [STT_.v()], [])
    fin = P.op("sp", [], [], lambda e: None)
    for ins in P.q["sp"]:
        if ins.dma_sem is sems["out"]:
            fin.deps.add(ins)

    for sname in ("ld", "x"):
        tot = 16 * P.dma_count.get(sems[sname], 0)
        for ins in P.q["sp"]:
            if ins.dma_sem is sems[sname]:
                ins.dma_val = tot

    if MODEL_REPORT and not SCHEDULE:
        print(f"[kernel] {len(P.all)} instructions; model makespan {P.simulate():.0f} us", flush=True)
    if SCHEDULE:
        base = P.simulate()
        est = P.schedule()
        chk = P.simulate()
        print(f"[kernel] {len(P.all)} instructions; model makespan recorded order {base:.0f} us -> list-scheduled {est:.0f} us (in-order replay {chk:.0f} us)", flush=True)

    eng_sem = {"pe": sems["pe"], "act": sems["act"], "dve": sems["dve"], "pool": sems["poolc"]}
    for e in ("pe", "act", "dve", "pool"):
        n = 0
        for ins in P.q[e]:
            if ins.flag and ins.dma_sem is None:
                n += 1
                ins.ms = n

    def run_queue(ename, eng):
        known = {}
        for ins in P.q[ename]:
            need = {}
            for d in ins.deps:
                if d.dma_sem is not None:
                    s, v = d.dma_sem, d.dma_val
                else:
                    s, v = eng_sem[d.eng], d.ms
                    assert v > 0
                if need.get(id(s), (None, 0))[1] < v:
                    need[id(s)] = (s, v)
            for sid, (s, v) in need.items():
                if known.get(sid, 0) >= v:
                    continue
                eng.wait_ge(s, v)
                known[sid] = v
            bi = ins.emit(eng)
            if ins.flag and ins.dma_sem is None:
                assert bi is not None
                bi.then_inc(eng_sem[ename], 1)

    with stack:
        with nc.Block() as block:
            @block.tensor
            def _(e):
                run_queue("pe", e)

            @block.scalar
            def _(e):
                run_queue("act", e)

            @block.vector
            def _(e):
                run_queue("dve", e)

            @block.gpsimd
            def _(e):
                run_queue("pool", e)

            @block.sync
            def _(e):
                run_queue("sp", e)
    return nc


def ffn_mod_bank_guard(steps):
    return steps


def to_fm(v):
    v = np.asarray(v, np.float32)
    lead = v.shape[:-1]
    n = v.shape[-1] // 128
    return np.ascontiguousarray(np.moveaxis(v.reshape(*lead, n, 128), -1, 0))


_NC_CACHE = {}


def kernel(x_prompt, x_sample, state_lru, c, c_ctx, mod_w, mod_b, norm_g,
           ab_w_in, lru_conv_w, lru_conv_b, lru_w_gates, lru_b_gates, lru_lambda,
           conf_conv_w, conf_conv_b, conf_ln_g, conf_ln_b, ab_w_out,
           c_w_in, c_ln_g, c_ln_b, c_w_s, c_b_s, c_w_out,
           ffn_w_up, ffn_conv_w, ffn_conv_b, ffn_w_down, _n_layers=4):
    f = lambda a: np.ascontiguousarray(np.asarray(a, np.float32))
    x_prompt, x_sample, state_lru, c, c_ctx = map(f, (x_prompt, x_sample, state_lru, c, c_ctx))
    shared = {
        "norm_g": to_fm(norm_g), "mod_b": to_fm(mod_b), "lru_cw": to_fm(lru_conv_w), "lru_cb": to_fm(lru_conv_b),
        "lru_bg": to_fm(lru_b_gates), "lru_lam": to_fm(lru_lambda), "cf_cw": to_fm(conf_conv_w),
        "cf_cb": to_fm(conf_conv_b), "cf_g": to_fm(conf_ln_g), "cf_b": to_fm(conf_ln_b),
        "c_lng": to_fm(c_ln_g), "f_cw": to_fm(ffn_conv_w), "f_cb": to_fm(ffn_conv_b),
    }
    ident = np.eye(128, dtype=np.float32)
    wst = np.ascontiguousarray(np.transpose(f(c_w_s), (0, 3, 1, 2)))
    def tile_cols(w, col_groups):
        L, K, _ = w.shape
        kc = K // 128
        tiles = []
        for groups_ in col_groups:
            blk = np.concatenate([w[:, :, a:b] for a, b in groups_], axis=2)
            fw = blk.shape[2]
            tiles.append(blk.reshape(L, kc, 128, fw).transpose(0, 2, 1, 3).reshape(L, 128, kc * fw))
        return np.ascontiguousarray(np.stack(tiles, axis=1))

    ab_in, c_in, up_w = f(ab_w_in), f(c_w_in), f(ffn_w_up)
    common = {
        "ident": ident,
        "mod_w": tile_cols(f(mod_w), [[(256 * t, 256 * t + 256)] for t in range(24)]),
        "ab_g": tile_cols(ab_in, [[(256 * j, 256 * j + 256)] for j in range(4)]),
        "ab_x": tile_cols(ab_in, [[(1024 + 128 * c_, 1024 + 128 * c_ + 128)] for c_ in range(8)]),
        "ab_glu": tile_cols(ab_in, [[(2048 + 128 * c_, 2048 + 128 * c_ + 128), (3072 + 128 * c_, 3072 + 128 * c_ + 128)] for c_ in range(8)]),
        "ab_w_out": tile_cols(f(ab_w_out), [[(128 * q, 128 * q + 128)] for q in range(8)]),
        "c_v": tile_cols(c_in, [[(2048 + 256 * q, 2048 + 256 * q + 256)] for q in range(8)]),
        "c_u": tile_cols(c_in, [[(256 * q, 256 * q + 256)] for q in range(8)]),
        "c_w_out": tile_cols(f(c_w_out), [[(128 * q, 128 * q + 128)] for q in range(8)]),
        "ffn_w_up": tile_cols(up_w, [[(128 * q, 128 * q + 128), (2816 + 128 * q, 2816 + 128 * q + 128)] for q in range(22)]),
        "ffn_w_down": tile_cols(f(ffn_w_down), [[(128 * q, 128 * q + 128)] for q in range(8)]),
        "lru_wg": f(lru_w_gates), "wst": wst, "c_bs": f(c_b_s), "c_lnb": f(c_ln_b),
    }
    prompts_of = []
    in_maps = []
    for core in range(8):
        if core < 4:
            pr = [2 * core, 2 * core + 1]
            xs = np.concatenate([x_sample[core]] + [x_prompt[p] for p in pr], axis=0)
            condS = c[core]
            h0 = state_lru[core]
            m = [1.0, 1.0, 1.0, 0.0, 0.0]
            pm = 1.0
        else:
            pr = list(range(8 + 6 * (core - 4), 8 + 6 * (core - 4) + 6))
            xs = np.concatenate([x_prompt[p] for p in pr], axis=0)
            condS = c_ctx
            h0 = np.zeros((2, 2, 1024), np.float32)
            m = [0.0] * 5
            pm = 0.0
        prompts_of.append(pr)
        pvd = dict(shared)
        pvd["h0"] = to_fm(h0)
        pvd["cond"] = np.ascontiguousarray(np.moveaxis(to_fm(np.stack([condS, c_ctx], 0)), 1, 2))
        pvd["pm"] = np.full((128, 1), pm, np.float32)
        pv = np.concatenate([pvd[n].reshape(128, -1) for n, _ in PV_SPEC], axis=1).astype(np.float32)
        assert pv.shape == (128, NPV), pv.shape
        mask = np.ascontiguousarray(np.broadcast_to(np.asarray(m, np.float32)[None, :, None], (128, 5, 16)).reshape(128, 80))
        d = dict(common)
        d.update({"xT": np.ascontiguousarray(xs.T), "pv": pv, "mask": mask})
        in_maps.append(d)
    if _n_layers not in _NC_CACHE:
        _NC_CACHE[_n_layers] = build(_n_layers)
    nc = _NC_CACHE[_n_layers]
    res = run_bass_kernel_spmd(nc, in_maps, core_ids=list(range(8)))
    y_prompt = np.zeros((32, 256, 1024), np.float32)
    y_sample = np.zeros((4, 1024, 1024), np.float32)
    new_state = np.zeros((32, 2, 2, 1024), np.float32)
    for core in range(8):
        y = np.asarray(res.results[core]["yT"], np.float32).T
        st = np.asarray(res.results[core]["st"], np.float32).reshape(2, 2, 6, 1024)
        if core < 4:
            y_sample[core] = y[:1024]
            segs = [4, 5]
        else:
            segs = list(range(6))
        for s, p in zip(segs, prompts_of[core]):
            y_prompt[p] = y[s * 256:(s + 1) * 256]
            new_state[p] = st[:, :, s, :]
    return (y_prompt, y_sample, new_state)
```
